# Optimizing a Trainium2 kernel written in Bass

```python
import math
import jax, jax.numpy as jnp
from jax import lax
import numpy as np

D_MODEL = 2048
BATCH = 4
SEQ = 4096
DEPTH = 4

GRID_W = 64
CTX_LEN = 256
N_MIXERS = 2
HEAD_DIM = 128
N_HEADS = D_MODEL // HEAD_DIM
N_KV_HEADS = 4
WINDOW = 128
BLOCK = WINDOW
ROPE_BASE = 10000.0
D_RNN = ((4 * D_MODEL // 3 + 255) // 256) * 256
RNN_BLOCKS = 16
RNN_BW = D_RNN // RNN_BLOCKS
CONV_W = 4
CONV_LEFT = CONV_W // 2
RG_C = 8.0
FFN_HIDDEN = ((8 * D_MODEL // 3 + 255) // 256) * 256
N_REC_LAYERS = (DEPTH + 1) // 2
N_ATT_LAYERS = DEPTH // 2
LN_EPS = 1e-5
NEG_INF = -1e30
MOD_INIT = 0.5

kernel_name = 'hybrid_rglru_swa_deepnorm_prefix_ctx'


def layer_norm(h, g, b):
    hf = h.astype(jnp.float32)
    mu = hf.mean(-1, keepdims=True)
    var = jnp.square(hf - mu).mean(-1, keepdims=True)
    return ((hf - mu) * lax.rsqrt(var + LN_EPS) * g + b).astype(h.dtype)


def modulate(h, shift, scale):
    return h * (1.0 + scale) + shift


def swiglu(u, w_gate, w_up, w_down):
    return (jax.nn.silu(u @ w_gate) * (u @ w_up)) @ w_down


def axial_rope_tables(n_tokens):
    rows = n_tokens // GRID_W
    row = jnp.repeat(jnp.arange(rows, dtype=jnp.float32), GRID_W)
    col = jnp.tile(jnp.arange(GRID_W, dtype=jnp.float32), rows)
    axis_dim = HEAD_DIM // 2
    inv_freq = ROPE_BASE ** (-jnp.arange(0, axis_dim, 2, dtype=jnp.float32) / axis_dim)
    ang = jnp.concatenate([row[:, None] * inv_freq, col[:, None] * inv_freq], axis=-1)
    return jnp.cos(ang), jnp.sin(ang)


def apply_rope(t, cos, sin):
    half = t.shape[-1] // 2
    t1, t2 = t[..., :half], t[..., half:]
    c, s = cos[:, None, :], sin[:, None, :]
    return jnp.concatenate([t1 * c - t2 * s, t2 * c + t1 * s], axis=-1).astype(t.dtype)


def sink_softmax(scores, sink):
    m = sink
    for s in scores:
        m = jnp.maximum(m, s.max(-1, keepdims=True))
    ps = [jnp.exp(s - m) for s in scores]
    denom = jnp.exp(sink - m) + sum(p.sum(-1, keepdims=True) for p in ps)
    return [p / denom for p in ps]


def windowed_latent_attention(q, k, v, kc, vc, sink):
    B, S, H, Dh = q.shape
    NB = S // BLOCK
    G = H // N_KV_HEADS
    scale = Dh ** -0.5
    qb = q.reshape(B, NB, BLOCK, N_KV_HEADS, G, Dh)

    def band(t):
        tp = jnp.pad(t, ((0, 0), (BLOCK, BLOCK), (0, 0), (0, 0)))
        return jnp.concatenate([tp[:, o * BLOCK:o * BLOCK + S].reshape(B, NB, BLOCK, N_KV_HEADS, Dh) for o in range(3)], axis=2)

    kb, vb = band(k), band(v)
    qi = jnp.arange(BLOCK)[:, None]
    si = jnp.arange(3 * BLOCK)[None, :]
    in_window = jnp.abs(si - BLOCK - qi) <= WINDOW
    key_pos = jnp.arange(NB)[:, None] * BLOCK - BLOCK + jnp.arange(3 * BLOCK)[None, :]
    in_range = (key_pos >= 0) & (key_pos < S)
    mask = in_window[None] & in_range[:, None, :]
    s_loc = jnp.einsum('bnqkgd,bnskd->bnkgqs', qb, kb).astype(jnp.float32) * scale
    s_loc = jnp.where(mask[None, :, None, None], s_loc, NEG_INF)
    s_ctx = jnp.einsum('bnqkgd,bckd->bnkgqc', qb, kc).astype(jnp.float32) * scale
    sink_b = sink.astype(jnp.float32).reshape(1, 1, N_KV_HEADS, G, 1, 1)
    p_loc, p_ctx = sink_softmax([s_loc, s_ctx], sink_b)
    o = (jnp.einsum('bnkgqs,bnskd->bnqkgd', p_loc.astype(v.dtype), vb)
         + jnp.einsum('bnkgqc,bckd->bnqkgd', p_ctx.astype(v.dtype), vc))
    return o.reshape(B, S, H * Dh)


def context_attention(qc, kc, vc, sink):
    B, C, H, Dh = qc.shape
    G = H // N_KV_HEADS
    qg = qc.reshape(B, C, N_KV_HEADS, G, Dh)
    s = jnp.einsum('bqkgd,bskd->bkgqs', qg, kc).astype(jnp.float32) * (Dh ** -0.5)
    (p,) = sink_softmax([s], sink.astype(jnp.float32).reshape(1, N_KV_HEADS, G, 1, 1))
    o = jnp.einsum('bkgqs,bskd->bqkgd', p.astype(vc.dtype), vc)
    return o.reshape(B, C, H * Dh)


def attention_mixer(uc, ux, w_qkv, sink, w_o, cos, sin, ctx_out):
    B, S, _ = ux.shape
    C = uc.shape[1]
    nq = N_HEADS * HEAD_DIM
    nkv = N_KV_HEADS * HEAD_DIM
    qkv = ux @ w_qkv
    q = apply_rope(qkv[..., :nq].reshape(B, S, N_HEADS, HEAD_DIM), cos, sin)
    k = apply_rope(qkv[..., nq:nq + nkv].reshape(B, S, N_KV_HEADS, HEAD_DIM), cos, sin)
    v = qkv[..., nq + nkv:].reshape(B, S, N_KV_HEADS, HEAD_DIM)
    if ctx_out:
        qkv_c = uc @ w_qkv
        qc = qkv_c[..., :nq].reshape(B, C, N_HEADS, HEAD_DIM)
        kvc = qkv_c[..., nq:]
    else:
        kvc = uc @ w_qkv[:, nq:]
    kc = kvc[..., :nkv].reshape(B, C, N_KV_HEADS, HEAD_DIM)
    vc = kvc[..., nkv:].reshape(B, C, N_KV_HEADS, HEAD_DIM)
    out_x = windowed_latent_attention(q, k, v, kc, vc, sink) @ w_o
    if not ctx_out:
        return None, out_x
    out_c = context_attention(qc, kc, vc, sink) @ w_o
    return out_c, out_x


def centred_dwconv(z, w, b):
    L = z.shape[1]
    zp = jnp.pad(z, ((0, 0), (CONV_LEFT, CONV_W - 1 - CONV_LEFT), (0, 0)))
    return sum(zp[:, j:j + L] * w[j] for j in range(CONV_W)) + b


def rglru_coeffs(z, wa, ba, wx, bx, lam):
    B, L, _ = z.shape
    zb = z.reshape(B, L, RNN_BLOCKS, RNN_BW)
    r = jax.nn.sigmoid((jnp.einsum('blhi,hij->blhj', zb, wa).reshape(B, L, D_RNN) + ba).astype(jnp.float32))
    ig = jax.nn.sigmoid((jnp.einsum('blhi,hij->blhj', zb, wx).reshape(B, L, D_RNN) + bx).astype(jnp.float32))
    log_a = -RG_C * r * jax.nn.softplus(-lam.astype(jnp.float32))
    a = jnp.exp(log_a)
    b = jnp.sqrt(-jnp.expm1(2.0 * log_a)) * ig * z.astype(jnp.float32)
    return a, b


def _linear_combine(left, right):
    a_l, b_l = left
    a_r, b_r = right
    return a_l * a_r, a_r * b_l + b_r


def linear_scan(a, b, h0, reverse):
    if h0 is not None:
        edge = -1 if reverse else 0
        b = b.at[:, edge].add(a[:, edge] * h0)
    return lax.associative_scan(_linear_combine, (a, b), reverse=reverse, axis=1)[1]


def recurrent_mixer(uc, ux, w_in, conv_w, conv_b, ga_w, ga_b, gx_w, gx_b, lam, w_out, ctx_out):
    px = ux @ w_in
    yx = jax.nn.gelu(px[..., :D_RNN])
    zx = centred_dwconv(px[..., D_RNN:], conv_w, conv_b)
    if ctx_out:
        pc = uc @ w_in
        yc = jax.nn.gelu(pc[..., :D_RNN])
        zc = centred_dwconv(pc[..., D_RNN:], conv_w, conv_b)
    else:
        zc = centred_dwconv(uc @ w_in[:, D_RNN:], conv_w, conv_b)
    hx = 0.0
    hc = 0.0
    for d, reverse in enumerate((False, True)):
        ac, bc = rglru_coeffs(zc, ga_w[d], ga_b[d], gx_w[d], gx_b[d], lam[d])
        h_ctx = linear_scan(ac, bc, None, reverse)
        h_end = h_ctx[:, 0] if reverse else h_ctx[:, -1]
        ax, bx_ = rglru_coeffs(zx, ga_w[d], ga_b[d], gx_w[d], gx_b[d], lam[d])
        hx = hx + linear_scan(ax, bx_, h_end, reverse)
        if ctx_out:
            hc = hc + h_ctx
    out_x = (yx * hx.astype(yx.dtype)) @ w_out
    if not ctx_out:
        return None, out_x
    out_c = (yc * hc.astype(yc.dtype)) @ w_out
    return out_c, out_x


def setup_inputs(seed: int = 0) -> dict:
    key = jax.random.key(seed)
    ks = iter(jax.random.split(key, 32))
    f32 = jnp.float32

    def nrm(shape, scale):
        return jax.random.normal(next(ks), shape, f32) * scale

    beta = (8.0 * DEPTH) ** -0.25
    D = D_MODEL
    u = jax.random.uniform(next(ks), (N_REC_LAYERS, 2, D_RNN), f32, minval=0.9, maxval=0.999)
    return {
        'x': nrm((BATCH, SEQ, D), 1.0),
        'c': nrm((BATCH, D), 1.0),
        'ctx': nrm((BATCH, CTX_LEN, D), 1.0),
        'c_ctx': nrm((D,), 1.0),
        'mod_w': nrm((DEPTH, D, 6 * D), MOD_INIT * D ** -0.5),
        'mod_b': nrm((DEPTH, 6 * D), 0.02),
        'ln_mix_g': 1.0 + nrm((DEPTH, D), 0.02),
        'ln_mix_b': nrm((DEPTH, D), 0.02),
        'ln_ffn_g': 1.0 + nrm((DEPTH, D), 0.02),
        'ln_ffn_b': nrm((DEPTH, D), 0.02),
        'ffn_w_gate': nrm((DEPTH, D, FFN_HIDDEN), D ** -0.5),
        'ffn_w_up': nrm((DEPTH, D, FFN_HIDDEN), D ** -0.5),
        'ffn_w_down': nrm((DEPTH, FFN_HIDDEN, D), beta * FFN_HIDDEN ** -0.5),
        'rec_w_in': nrm((N_REC_LAYERS, D, 2 * D_RNN), D ** -0.5),
        'rec_conv_w': nrm((N_REC_LAYERS, CONV_W, D_RNN), CONV_W ** -0.5),
        'rec_conv_b': nrm((N_REC_LAYERS, D_RNN), 0.02),
        'rec_gate_a_w': nrm((N_REC_LAYERS, 2, RNN_BLOCKS, RNN_BW, RNN_BW), RNN_BW ** -0.5),
        'rec_gate_a_b': nrm((N_REC_LAYERS, 2, D_RNN), 0.02),
        'rec_gate_x_w': nrm((N_REC_LAYERS, 2, RNN_BLOCKS, RNN_BW, RNN_BW), RNN_BW ** -0.5),
        'rec_gate_x_b': nrm((N_REC_LAYERS, 2, D_RNN), 0.02),
        'rec_lambda': jnp.log(u) - jnp.log1p(-u),
        'rec_w_out': nrm((N_REC_LAYERS, D_RNN, D), beta * D_RNN ** -0.5),
        'att_w_qkv': nrm((N_ATT_LAYERS, D, (N_HEADS + 2 * N_KV_HEADS) * HEAD_DIM), D ** -0.5),
        'att_sink': nrm((N_ATT_LAYERS, N_HEADS), 0.5),
        'att_w_o': nrm((N_ATT_LAYERS, N_HEADS * HEAD_DIM, D), beta * (N_HEADS * HEAD_DIM) ** -0.5),
    }


def reference(x, c, ctx, c_ctx, mod_w, mod_b, ln_mix_g, ln_mix_b, ln_ffn_g, ln_ffn_b,
              ffn_w_gate, ffn_w_up, ffn_w_down, rec_w_in, rec_conv_w, rec_conv_b,
              rec_gate_a_w, rec_gate_a_b, rec_gate_x_w, rec_gate_x_b, rec_lambda, rec_w_out,
              att_w_qkv, att_sink, att_w_o):
    S = x.shape[1]
    cos, sin = axial_rope_tables(S)
    alpha = (2.0 * DEPTH) ** 0.25
    hx, hc = x, ctx
    for i in range(DEPTH):
        last = i == DEPTH - 1
        j = i // N_MIXERS
        mod_x = jax.nn.silu(c) @ mod_w[i] + mod_b[i]
        mod_c = jax.nn.silu(c_ctx) @ mod_w[i] + mod_b[i]
        sh1x, sc1x, g1x, sh2x, sc2x, g2x = jnp.split(mod_x[:, None, :], 6, axis=-1)
        sh1c, sc1c, g1c, sh2c, sc2c, g2c = jnp.split(mod_c, 6)
        ux = modulate(hx, sh1x, sc1x)
        uc = modulate(hc, sh1c, sc1c)
        if i % N_MIXERS == 0:
            oc, ox = recurrent_mixer(uc, ux, rec_w_in[j], rec_conv_w[j], rec_conv_b[j],
                                     rec_gate_a_w[j], rec_gate_a_b[j], rec_gate_x_w[j], rec_gate_x_b[j],
                                     rec_lambda[j], rec_w_out[j], not last)
        else:
            oc, ox = attention_mixer(uc, ux, att_w_qkv[j], att_sink[j], att_w_o[j], cos, sin, not last)
        hx = layer_norm(alpha * hx + g1x * ox, ln_mix_g[i], ln_mix_b[i])
        hx = layer_norm(alpha * hx + g2x * swiglu(modulate(hx, sh2x, sc2x), ffn_w_gate[i], ffn_w_up[i], ffn_w_down[i]),
                        ln_ffn_g[i], ln_ffn_b[i])
        if not last:
            hc = layer_norm(alpha * hc + g1c * oc, ln_mix_g[i], ln_mix_b[i])
            hc = layer_norm(alpha * hc + g2c * swiglu(modulate(hc, sh2c, sc2c), ffn_w_gate[i], ffn_w_up[i], ffn_w_down[i]),
                            ln_ffn_g[i], ln_ffn_b[i])
    return hx
```

```python
import math
from contextlib import ExitStack

import numpy as np
import concourse.bass as bass
import concourse.mybir as mybir
from concourse.bass_utils import run_bass_kernel_spmd

F32 = mybir.dt.float32
BF16 = mybir.dt.bfloat16
ALU = mybir.AluOpType
AF = mybir.ActivationFunctionType
LN_EPS = 1e-5
LN_FP32R = False
STORES_ON_ACT = False
RG_C = 8.0


class Cfg:
    def __init__(self, D=2048, S=4096, C=256, Dr=2816, NBLK=16, F=5632, HD=128, KV=4,
                 depth=4, grid_w=64, rope_base=10000.0):
        self.D, self.S, self.C, self.Dr, self.NBLK, self.F = D, S, C, Dr, NBLK, F
        self.HD, self.KV, self.depth = HD, KV, depth
        self.H = D // HD
        self.G = self.H // KV
        self.BW = Dr // NBLK
        self.DC, self.RC, self.FC = D // 128, Dr // 128, F // 128
        self.T = C + S
        self.NQKV = (self.H + 2 * KV) * HD
        self.grid_w, self.rope_base = grid_w, rope_base
        self.alpha = (2.0 * depth) ** 0.25
        self.n_rec = (depth + 1) // 2
        self.n_att = depth // 2
        self.tiles = [(0, C)] + [(C + 512 * j, 512) for j in range(S // 512)]

    def sups(self, maxtok):
        out, cur, n = [], [], 0
        for i, (_, t) in enumerate(self.tiles):
            if cur and n + t > maxtok:
                out.append(cur)
                cur, n = [], 0
            cur.append(i)
            n += t
        if cur:
            out.append(cur)
        return out


class Buf:
    __slots__ = ("w", "r")

    def __init__(self):
        self.w = {}
        self.r = {}


class Eng:
    def __init__(self, e, sem, sid, name):
        self.e, self.sem, self.sid, self.name = e, sem, sid, name
        self.count = 0
        self.waited = {}


class FW:
    def __init__(self, nc, stack, n_dma_sems=24):
        self.nc = nc
        self.sems = []

        def mk(name):
            s = stack.enter_context(nc.semaphore(name))
            self.sems.append(s)
            return s, len(self.sems) - 1

        self.pe = Eng(nc.tensor, *mk("c_pe"), "pe")
        self.act = Eng(nc.scalar, *mk("c_act"), "act")
        self.dve = Eng(nc.vector, *mk("c_dve"), "dve")
        self.pool = Eng(nc.gpsimd, *mk("c_pool"), "pool")
        self.sp = Eng(nc.sync, None, None, "sp")
        self.engs = [self.pe, self.act, self.dve, self.pool, self.sp]
        self.dsems = []
        for i in range(n_dma_sems):
            s, sid = mk("d%d" % i)
            self.dsems.append([s, sid, 0])
        self.dnext = 0
        self.bar_sem, self.bar_sid = mk("bar")
        self.bar_cnt = 0
        self.bar_src = self.bar_dst = None

    def _wait(self, eng, sid, val):
        if val <= 0 or eng.waited.get(sid, 0) >= val:
            return
        eng.e.wait_ge(self.sems[sid], val)
        eng.waited[sid] = val

    def _deps(self, eng, reads, writes):
        deps = {}
        for b in reads:
            for s, v in b.w.items():
                if deps.get(s, 0) < v:
                    deps[s] = v
        for b in writes:
            for s, v in b.w.items():
                if deps.get(s, 0) < v:
                    deps[s] = v
            for s, v in b.r.items():
                if deps.get(s, 0) < v:
                    deps[s] = v
        for s, v in deps.items():
            if eng is self.pe and s == self.pe.sid:
                continue
            self._wait(eng, s, v)

    def _mark(self, sid, val, reads, writes):
        for b in reads:
            if b.r.get(sid, 0) < val:
                b.r[sid] = val
        for b in writes:
            b.w = {sid: val}
            b.r = {}

    def op(self, eng, fn, reads=(), writes=()):
        self._deps(eng, reads, writes)
        ins = fn()
        eng.count += 1
        ins.then_inc(eng.sem, 1)
        self._mark(eng.sid, eng.count, reads, writes)
        return ins

    def dma(self, out, in_, reads=(), writes=(), **kw):
        q = self.act if (STORES_ON_ACT and str(out.space).endswith("DRAM")) else self.sp
        d = self.dsems[self.dnext]
        self.dnext = (self.dnext + 1) % len(self.dsems)
        self._wait(q, d[1], d[2])
        self._deps(q, reads, writes)
        ins = q.e.dma_start(out=out, in_=in_, **kw)
        d[2] += 16
        ins.then_inc(d[0], 16)
        self._mark(d[1], d[2], reads, writes)
        return ins

    def barrier(self):
        sp = self.sp
        for e in (self.pe, self.act, self.dve, self.pool):
            self._wait(sp, e.sid, e.count)
        for d in self.dsems:
            self._wait(sp, d[1], d[2])
        ins = sp.e.dma_start(out=self.bar_dst, in_=self.bar_src)
        self.bar_cnt += 16
        ins.then_inc(self.bar_sem, 16)
        for e in self.engs:
            self._wait(e, self.bar_sid, self.bar_cnt)


class Rot:
    def __init__(self, kb, st, name, shape, dt, count):
        self.items = [(st.enter_context(kb.nc.sbuf_tensor("%s_%d_%d" % (name, kb.uid(), i), shape, dt)), Buf())
                      for i in range(count)]
        self.i = 0

    def next(self):
        it = self.items[self.i]
        self.i = (self.i + 1) % len(self.items)
        return it


class KB:
    def __init__(self, cfg):
        self.cfg = cfg
        self._uid = 0
        self.nc = bass.Bass("TRN2", target_bir_lowering=False)

    def uid(self):
        self._uid += 1
        return self._uid

    def sb(self, st, name, shape, dt=F32):
        return st.enter_context(self.nc.sbuf_tensor("%s_%d" % (name, self.uid()), shape, dt))

    def build(self):
        cfg, nc = self.cfg, self.nc
        D, S, C, T, Dr, F, L = cfg.D, cfg.S, cfg.C, cfg.T, cfg.Dr, cfg.F, cfg.depth
        self.din = {}

        def I(name, shape):
            self.din[name] = nc.dram_tensor(name, list(shape), F32, kind="ExternalInput").ap()

        for name, shape in input_shapes(cfg).items():
            I(name, shape)
        self.out = nc.dram_tensor("out", [S, D], F32, kind="ExternalOutput").ap()

        def scratch(name, rows, dt):
            return nc.dram_tensor(name, [rows, T], dt).ap().rearrange("(c p) t -> p c t", p=128)

        self.HA = scratch("s_ha", D, F32)
        self.R = scratch("s_r", D, F32)
        self.Y = scratch("s_y", Dr, BF16)
        self.ZP = scratch("s_zp", Dr, F32)
        self.Z = scratch("s_z", Dr, F32)
        self.ZB = scratch("s_zb", Dr, BF16)
        self.M = scratch("s_m", Dr, BF16)
        self.Q = scratch("s_q", D, BF16)
        self.O = scratch("s_o", D, BF16)
        self.AFF = scratch("s_af", F, BF16)

        with ExitStack() as st:
            self.fw = fw = FW(nc, st)
            b0 = self.sb(st, "bar0", [1, 8])
            b1 = self.sb(st, "bar1", [1, 8])
            fw.bar_src, fw.bar_dst = b0[:], b1[:]
            self.PS = [st.enter_context(nc.psum_tensor("ps%d" % i, [128, 512], F32)) for i in range(8)]
            self.PSB = [Buf() for _ in range(8)]
            self.bank_i = 0
            self.ident = self.sb(st, "ident", [128, 128])
            self.identB = Buf()
            fw.dma(self.ident[:], self.din["ident"][:, :], writes=[self.identB])
            self.ones_f = self.sb(st, "ones_f", [128, 128])
            self.ones_b = self.sb(st, "ones_b", [128, 128], BF16)
            self.onesB = Buf()
            fw.op(fw.pool, lambda: nc.gpsimd.memset(self.ones_f[:], 1.0), writes=[self.onesB])
            fw.op(fw.pool, lambda: nc.gpsimd.memset(self.ones_b[:], 1.0), writes=[self.onesB])
            self.colst = self.sb(st, "colst", [128, 128])
            self.colstB = Buf()
            self.silc = self.sb(st, "silc", [128, cfg.DC, 2])
            self.silcB = Buf()
            tmpc = self.sb(st, "tmpc", [128, cfg.DC])
            tmpcB = Buf()
            for j, nm in enumerate(("c", "c_ctx")):
                self.load_cols(tmpc[:, :], tmpcB, self.din[nm][0, :].rearrange("(n p) -> n p", p=128), cfg.DC)
                fw.op(fw.act, lambda: nc.scalar.activation(out=self.silc[:, :, j], in_=tmpc[:, :], func=AF.Silu),
                      reads=[tmpcB], writes=[self.silcB])
            fw.barrier()
            self.phase_in()
            for l in range(L):
                with ExitStack() as lst:
                    self.layer(l, lst)
            self.phase_out()
            fw.barrier()
        return nc

    def next_bank(self):
        b = self.bank_i
        self.bank_i = (self.bank_i + 1) % 8
        return b

    def load_cols(self, dst, dstB, src2d, n, scale=None):
        fw, nc = self.fw, self.nc
        fw.dma(self.colst[0:n, :], src2d, writes=[self.colstB])
        b = self.next_bank()
        fw.op(fw.pe, lambda: nc.tensor.transpose(self.PS[b][:, 0:n], self.colst[0:n, :], self.ident[0:n, 0:n]),
              reads=[self.colstB, self.identB], writes=[self.PSB[b]])
        if scale is None:
            fw.op(fw.act, lambda: nc.scalar.copy(out=dst, in_=self.PS[b][:, 0:n]), reads=[self.PSB[b]], writes=[dstB])
        else:
            fw.op(fw.act, lambda: nc.scalar.mul(out=dst, in_=self.PS[b][:, 0:n], mul=scale),
                  reads=[self.PSB[b]], writes=[dstB])

    def phase_in(self):
        cfg, fw, nc = self.cfg, self.fw, self.nc
        DC = cfg.DC
        with ExitStack() as st:
            xin = Rot(self, st, "xin", [128, cfg.D], F32, 3)
            stg = Rot(self, st, "xstg", [128, DC, 128], F32, 2)
            blocks = [("ctx", r, r) for r in range(0, cfg.C, 128)] + [("x", r, cfg.C + r) for r in range(0, cfg.S, 128)]
            xs_ = {}

            def ldx(i_):
                nm_, r0_, _ = blocks[i_]
                xs_[i_] = xin.next()
                fw.dma(xs_[i_][0][:], self.din[nm_][r0_:r0_ + 128, :], writes=[xs_[i_][1]])

            ldx(0)
            for bi_, (nm, r0, t0) in enumerate(blocks):
                if bi_ + 1 < len(blocks):
                    ldx(bi_ + 1)
                xt, xB = xs_.pop(bi_)
                sg, sB = stg.next()
                for g0 in range(0, DC, 4):
                    gn = min(4, DC - g0)
                    b = self.next_bank()
                    for i in range(gn):
                        fw.op(fw.pe, lambda: nc.tensor.transpose(self.PS[b][:, i * 128:(i + 1) * 128],
                                                                  xt[:, (g0 + i) * 128:(g0 + i + 1) * 128], self.ident[:]),
                              reads=[xB, self.identB], writes=[self.PSB[b]])
                    fw.op(fw.act, lambda: nc.scalar.mul(out=sg[:, g0:g0 + gn, :].rearrange("p a b -> p (a b)"),
                                                        in_=self.PS[b][:, 0:gn * 128], mul=cfg.alpha),
                          reads=[self.PSB[b]], writes=[sB])
                fw.dma(self.HA[:, :, t0:t0 + 128], sg[:], reads=[sB])
            fw.barrier()

    def phase_out(self):
        cfg, fw, nc = self.cfg, self.fw, self.nc
        DC = cfg.DC
        with ExitStack() as st:
            hin = Rot(self, st, "hin", [128, DC, 128], F32, 3)
            ost = Rot(self, st, "ost", [128, cfg.D], F32, 2)
            hs_ = {}

            def ldh(r_):
                hs_[r_] = hin.next()
                fw.dma(hs_[r_][0][:], self.HA[:, :, cfg.C + r_:cfg.C + r_ + 128], writes=[hs_[r_][1]])

            ldh(0)
            for r0 in range(0, cfg.S, 128):
                t0 = cfg.C + r0
                if r0 + 128 < cfg.S:
                    ldh(r0 + 128)
                ht, hB = hs_.pop(r0)
                ot, oB = ost.next()
                for g0 in range(0, DC, 4):
                    gn = min(4, DC - g0)
                    b = self.next_bank()
                    for i in range(gn):
                        fw.op(fw.pe, lambda: nc.tensor.transpose(self.PS[b][:, i * 128:(i + 1) * 128],
                                                                  ht[:, g0 + i, :], self.ident[:]),
                              reads=[hB, self.identB], writes=[self.PSB[b]])
                    fw.op(fw.act, lambda: nc.scalar.copy(out=ot[:, g0 * 128:(g0 + gn) * 128],
                                                         in_=self.PS[b][:, 0:gn * 128]),
                          reads=[self.PSB[b]], writes=[oB])
                fw.dma(self.out[r0:r0 + 128, :], ot[:], reads=[oB])

    def layer(self, l, lst):
        cfg, fw, nc = self.cfg, self.fw, self.nc
        DC = cfg.DC
        last = l == cfg.depth - 1
        j = l // 2
        is_rec = (l % 2 == 0)
        self.modv = self.sb(lst, "modv", [128, 6 * DC, 2])
        self.modvB = Buf()
        self.phase_mod(l)
        self.ln = self.sb(lst, "lncols", [128, 4, DC])
        self.lnB = Buf()
        a2 = 1.0 if last else cfg.alpha
        for i, (nm, sc) in enumerate((("ln_mix_g", cfg.alpha), ("ln_mix_b", cfg.alpha), ("ln_ffn_g", a2), ("ln_ffn_b", a2))):
            self.load_cols(self.ln[:, i, :], self.lnB, self.din[nm][l, :].rearrange("(n p) -> n p", p=128), DC, scale=sc)
        fw.barrier()
        if is_rec:
            self.rec_mixer(j, lst)
        else:
            self.att_mixer(j, lst, last)
        self.phase_ln(0)
        self.ffn(l)
        self.phase_ln(1)

    def mcol(self, grp, kc, ti):
        return self.modv[:, grp * self.cfg.DC + kc, (1 if ti == 0 else 0):(2 if ti == 0 else 1)]

    def phase_mod(self, l):
        cfg, fw, nc = self.cfg, self.fw, self.nc
        DC = cfg.DC
        N = 6 * cfg.D
        NG = 512
        wv = self.din["mod_w"][l].rearrange("(kc p) n -> p kc n", p=128)
        with ExitStack() as st:
            wsl = Rot(self, st, "modw", [128, DC, NG], F32, 2)
            mb = self.sb(st, "modb", [128, 6 * DC])
            mbB = Buf()
            self.load_cols(mb[:, :], mbB, self.din["mod_b"][l, :].rearrange("(n p) -> n p", p=128), 6 * DC)
            rowb = self.sb(st, "modrow", [2, N])
            rowB = Buf()
            nxt = wsl.next()
            fw.dma(nxt[0][:], wv[:, :, 0:NG], writes=[nxt[1]])
            for g in range(N // NG):
                wt, wB = nxt
                if g + 1 < N // NG:
                    nxt = wsl.next()
                    fw.dma(nxt[0][:], wv[:, :, (g + 1) * NG:(g + 2) * NG], writes=[nxt[1]])
                b = self.next_bank()
                for kc in range(DC):
                    fw.op(fw.pe, lambda: nc.tensor.matmul(self.PS[b][0:2, 0:NG], lhsT=self.silc[:, kc, :], rhs=wt[:, kc, :],
                                                          start=(kc == 0), stop=(kc == DC - 1)),
                          reads=[wB, self.silcB], writes=[self.PSB[b]])
                fw.op(fw.act, lambda: nc.scalar.copy(out=rowb[0:2, g * NG:(g + 1) * NG], in_=self.PS[b][0:2, 0:NG]),
                      reads=[self.PSB[b]], writes=[rowB])
            pb = self.next_bank()
            psm = self.PS[pb]
            for ch in range(6 * DC):
                fw.op(fw.pe, lambda: nc.tensor.transpose(psm[:, 2 * ch:2 * ch + 2], rowb[0:2, ch * 128:(ch + 1) * 128], self.ident[0:2, 0:2]),
                      reads=[rowB, self.identB], writes=[self.PSB[pb]])
            for jx in range(2):
                fw.op(fw.dve, lambda: nc.vector.tensor_tensor(
                    out=self.modv[:, :, jx], in0=psm[:, 0:12 * DC].rearrange("p (a b) -> p a b", b=2)[:, :, jx],
                    in1=mb[:, :], op=ALU.add), reads=[self.PSB[pb], mbB], writes=[self.modvB])
            for grp in (1, 4):
                fw.op(fw.dve, lambda: nc.vector.tensor_scalar(
                    out=self.modv[:, grp * DC:(grp + 1) * DC, :], in0=self.modv[:, grp * DC:(grp + 1) * DC, :],
                    scalar1=1.0, scalar2=1.0 / cfg.alpha, op0=ALU.add, op1=ALU.mult),
                    reads=[self.modvB], writes=[self.modvB])
            fw.barrier()

    def linear(self, KC, wlist, N, NG, sups, load_act, epi, modeA=None, sup_begin=None, pre=None, nst=2):
        cfg, fw, nc = self.cfg, self.fw, self.nc
        nw = len(wlist)
        wv = [w.rearrange("(kc p) n -> p kc n", p=128) for w in wlist]
        tiles = cfg.tiles
        with ExitStack() as st:
            Tmax = max(sum(tiles[i][1] for i in sup) for sup in sups)
            act = self.sb(st, "lin_act", [128, KC, Tmax], BF16)
            actB = [Buf() for _ in range(max(len(s) for s in sups))]
            wst = [self.sb(st, "lin_wst", [128, KC, nw, NG], F32) for _ in range(nst)]
            wstB = [[Buf() for _ in range(nw)] for _ in range(nst)]
            wbf = [self.sb(st, "lin_wbf", [128, KC, nw, NG], BF16) for _ in range(2)]
            wbfB = [[Buf(), Buf()], [Buf(), Buf()]]
            ng = N // NG

            def load_w(g):
                bf = g % 2
                sf = g % nst
                for wi in range(nw):
                    fw.dma(wst[sf][:, :, wi, :], wv[wi][:, :, g * NG:(g + 1) * NG], writes=[wstB[sf][wi]])
                k1 = max(1, KC // 4)
                fw.op(fw.pool, lambda: nc.gpsimd.tensor_copy(out=wbf[bf][:, 0:k1].rearrange("p a b c -> p (a b c)"),
                                                              in_=wst[sf][:, 0:k1].rearrange("p a b c -> p (a b c)")),
                      reads=wstB[sf], writes=[wbfB[bf][0]])
                fw.op(fw.act, lambda: nc.scalar.copy(out=wbf[bf][:, k1:KC].rearrange("p a b c -> p (a b c)"),
                                                     in_=wst[sf][:, k1:KC].rearrange("p a b c -> p (a b c)")),
                      reads=wstB[sf], writes=[wbfB[bf][1]])

            for sup in sups:
                load_w(0)
                slots, off = [], 0
                for si, ti in enumerate(sup):
                    n = tiles[ti][1]
                    slots.append((ti, off, n, actB[si]))
                    load_act(ti, act, off, n, actB[si])
                    off += n
                if sup_begin is not None:
                    sup_begin(sup)
                for g in range(ng):
                    if g + 1 < ng:
                        load_w(g + 1)
                    bf = g % 2
                    if modeA is not None and modeA(g):
                        for (ti, off, n, aB) in slots:
                            for blk in range(n // 128):
                                b = self.next_bank()
                                for kc in range(KC):
                                    fw.op(fw.pe, lambda: nc.tensor.matmul(
                                        self.PS[b][:, 0:NG], lhsT=act[:, kc, off + blk * 128:off + (blk + 1) * 128],
                                        rhs=wbf[bf][:, kc, 0, :], start=(kc == 0), stop=(kc == KC - 1)),
                                        reads=[wbfB[bf][0], wbfB[bf][1], aB], writes=[self.PSB[b]])
                                epi(("A", g, blk), ti, [self.PS[b][:, 0:NG]], [self.PSB[b]])
                        continue
                    units = [(nn, sl) for nn in range(NG // 128) for sl in slots]
                    pres = {}
                    for ui_, (nn, (ti, off, n, aB)) in enumerate(units):
                        nchunk = g * (NG // 128) + nn
                        if pre is not None:
                            for k_ in (ui_, ui_ + 1):
                                if k_ < len(units) and k_ not in pres:
                                    pres[k_] = pre(g * (NG // 128) + units[k_][0], units[k_][1][0])
                        if True:
                            banks = [self.next_bank() for _ in range(nw)]
                            for wi in range(nw):
                                b = banks[wi]
                                for kc in range(KC):
                                    fw.op(fw.pe, lambda: nc.tensor.matmul(
                                        self.PS[b][:, 0:n], lhsT=wbf[bf][:, kc, wi, nn * 128:(nn + 1) * 128],
                                        rhs=act[:, kc, off:off + n], start=(kc == 0), stop=(kc == KC - 1)),
                                        reads=[wbfB[bf][0], wbfB[bf][1], aB], writes=[self.PSB[b]])
                            if pre is not None:
                                epi(nchunk, ti, [self.PS[b][:, 0:n] for b in banks], [self.PSB[b] for b in banks], pres.pop(ui_))
                            else:
                                epi(nchunk, ti, [self.PS[b][:, 0:n] for b in banks], [self.PSB[b] for b in banks])
            fw.barrier()

    def mk_mod_loader(self, st, g_shift, g_scale, div=2):
        cfg, fw, nc = self.cfg, self.fw, self.nc
        DC = cfg.DC
        hc = max(1, (DC + div - 1) // div)
        stage = Rot(self, st, "modst", [128, hc, 512], F32, 2)

        def load(ti, act, off, n, aB):
            t0 = cfg.tiles[ti][0]
            for c0 in range(0, DC, hc):
                cn = min(hc, DC - c0)
                sg, sB = stage.next()
                fw.dma(sg[:, 0:cn, 0:n], self.HA[:, c0:c0 + cn, t0:t0 + n], writes=[sB])
                for k in range(cn):
                    kc = c0 + k
                    fw.op(fw.act, lambda: nc.scalar.activation(
                        out=act[:, kc, off:off + n], in_=sg[:, k, 0:n], func=AF.Identity,
                        scale=self.mcol(g_scale, kc, ti), bias=self.mcol(g_shift, kc, ti)),
                        reads=[sB, self.modvB], writes=[aB])
        return load

    def mk_direct_loader(self, src, KC):
        fw = self.fw

        def load(ti, act, off, n, aB):
            t0 = self.cfg.tiles[ti][0]
            fw.dma(act[:, :, off:off + n], src[:, 0:KC, t0:t0 + n], writes=[aB])
        return load

    def mk_resid_epi(self, st, g_gate):
        cfg, fw, nc = self.cfg, self.fw, self.nc
        hin = Rot(self, st, "rs_h", [128, 512], F32, 4)
        rout = Rot(self, st, "rs_o", [128, 512], F32, 4)

        def pre(nchunk, ti):
            t0, n = cfg.tiles[ti]
            ht, hB = hin.next()
            fw.dma(ht[:, 0:n], self.HA[:, nchunk, t0:t0 + n], writes=[hB])
            return ht, hB

        def epi(nchunk, ti, ps, psB, pr_):
            t0, n = cfg.tiles[ti]
            ht, hB = pr_
            ot, oB = rout.next()
            fw.op(fw.dve, lambda: nc.vector.scalar_tensor_tensor(
                out=ot[:, 0:n], in0=ps[0], scalar=self.mcol(g_gate, nchunk, ti), in1=ht[:, 0:n],
                op0=ALU.mult, op1=ALU.add), reads=[psB[0], hB, self.modvB], writes=[oB])
            fw.dma(self.R[:, nchunk, t0:t0 + n], ot[:, 0:n], reads=[oB])
        return epi, pre

    def phase_ln(self, which):
        cfg, fw, nc = self.cfg, self.fw, self.nc
        DC, D = cfg.DC, cfg.D
        gcol, bcol = 2 * which, 2 * which + 1
        R32 = mybir.dt.float32r if LN_FP32R else F32
        with ExitStack() as st:
            rin = [(self.sb(st, "ln_r", [128, DC, 512]), [Buf() for _ in range(DC)]) for _ in range(2)]
            rout = [(self.sb(st, "ln_o", [128, DC, 512]), [Buf() for _ in range(DC)]) for _ in range(2)]
            sqt = self.sb(st, "ln_sq", [128, DC, 512])
            sqB = Buf()
            small = Rot(self, st, "ln_s", [128, 512], F32, 8)
            def ld(it):
                t0_, n_ = cfg.tiles[it]
                fw.dma(rin[it % 2][0][:, :, 0:n_], self.R[:, :, t0_:t0_ + n_], writes=rin[it % 2][1])

            ld(0)
            for it, (t0, n) in enumerate(cfg.tiles):
                rt, rBs = rin[it % 2]
                ot, oBs = rout[it % 2]
                if it + 1 < len(cfg.tiles):
                    ld(it + 1)
                fw.op(fw.act, lambda: nc.scalar.activation(out=sqt[:, :, 0:n], in_=rt[:, :, 0:n], func=AF.Square),
                      reads=rBs, writes=[sqB])
                b1, b2 = self.next_bank(), self.next_bank()
                for c in range(DC):
                    fw.op(fw.pe, lambda: nc.tensor.matmul(self.PS[b1][:, 0:n], lhsT=self.ones_f[:].bitcast(R32), rhs=rt[:, c, 0:n].bitcast(R32),
                                                          start=(c == 0), stop=(c == DC - 1)),
                          reads=[rBs[c], self.onesB], writes=[self.PSB[b1]])
                for c in range(DC):
                    fw.op(fw.pe, lambda: nc.tensor.matmul(self.PS[b2][:, 0:n], lhsT=self.ones_f[:].bitcast(R32), rhs=sqt[:, c, 0:n].bitcast(R32),
                                                          start=(c == 0), stop=(c == DC - 1)),
                          reads=[sqB, self.onesB], writes=[self.PSB[b2]])
                mean, mB = small.next()
                fw.op(fw.act, lambda: nc.scalar.mul(out=mean[:, 0:n], in_=self.PS[b1][:, 0:n], mul=1.0 / D),
                      reads=[self.PSB[b1]], writes=[mB])
                msq, qB = small.next()
                fw.op(fw.pool, lambda: nc.gpsimd.tensor_tensor(out=msq[:, 0:n], in0=mean[:, 0:n], in1=mean[:, 0:n], op=ALU.mult),
                      reads=[mB], writes=[qB])
                var, vB = small.next()
                fw.op(fw.dve, lambda: nc.vector.scalar_tensor_tensor(out=var[:, 0:n], in0=self.PS[b2][:, 0:n], scalar=1.0 / D,
                                                                      in1=msq[:, 0:n], op0=ALU.mult, op1=ALU.subtract),
                      reads=[self.PSB[b2], qB], writes=[vB])
                fw.op(fw.act, lambda: nc.scalar.activation(out=var[:, 0:n], in_=var[:, 0:n], func=AF.Sqrt, bias=LN_EPS, scale=1.0),
                      reads=[vB], writes=[vB])
                rstd, sB = small.next()
                fw.op(fw.dve, lambda: nc.vector.reciprocal(out=rstd[:, 0:n], in_=var[:, 0:n]), reads=[vB], writes=[sB])
                for c in range(DC):
                    eng = fw.dve if c % 2 == 0 else fw.pool
                    e = nc.vector if c % 2 == 0 else nc.gpsimd
                    fw.op(eng, lambda: e.tensor_tensor(out=rt[:, c, 0:n], in0=rt[:, c, 0:n], in1=mean[:, 0:n], op=ALU.subtract),
                          reads=[mB], writes=[rBs[c]])
                    fw.op(eng, lambda: e.tensor_tensor(out=rt[:, c, 0:n], in0=rt[:, c, 0:n], in1=rstd[:, 0:n], op=ALU.mult),
                          reads=[sB], writes=[rBs[c]])
                    fw.op(fw.act, lambda: nc.scalar.activation(out=ot[:, c, 0:n], in_=rt[:, c, 0:n], func=AF.Identity,
                                                               scale=self.ln[:, gcol, c:c + 1], bias=self.ln[:, bcol, c:c + 1]),
                          reads=[rBs[c], self.lnB], writes=[oBs[c]])
                fw.dma(self.HA[:, :, t0:t0 + n], ot[:, :, 0:n], reads=oBs)
            fw.barrier()

    def ffn(self, l):
        cfg, fw, nc = self.cfg, self.fw, self.nc
        with ExitStack() as st:
            loader = self.mk_mod_loader(st, 3, 4)
            sg = Rot(self, st, "ff_sg", [128, 512], F32, 3)
            ho = Rot(self, st, "ff_ho", [128, 512], BF16, 4)

            def epi(nchunk, ti, ps, psB):
                t0, n = cfg.tiles[ti]
                s, sB = sg.next()
                fw.op(fw.act, lambda: nc.scalar.activation(out=s[:, 0:n], in_=ps[0], func=AF.Silu), reads=[psB[0]], writes=[sB])
                h, hB = ho.next()
                fw.op(fw.dve, lambda: nc.vector.tensor_tensor(out=h[:, 0:n], in0=ps[1], in1=s[:, 0:n], op=ALU.mult),
                      reads=[psB[1], sB], writes=[hB])
                fw.dma(self.AFF[:, nchunk, t0:t0 + n], h[:, 0:n], reads=[hB])

            self.linear(cfg.DC, [self.din["ffn_w_gate"][l], self.din["ffn_w_up"][l]], cfg.F, 256 if cfg.F % 256 == 0 else 128, cfg.sups(2304), loader, epi, nst=1)
        with ExitStack() as st:
            epi, pre = self.mk_resid_epi(st, 5)
            self.linear(cfg.FC, [self.din["ffn_w_down"][l]], cfg.D, 128, cfg.sups(1536),
                        self.mk_direct_loader(self.AFF, cfg.FC), epi, pre=pre, nst=1)

    def rec_mixer(self, j, lst):
        cfg, fw, nc = self.cfg, self.fw, self.nc
        DC, RC, T, C, BW = cfg.DC, cfg.RC, cfg.T, cfg.C, cfg.BW
        with ExitStack() as st:
            loader = self.mk_mod_loader(st, 0, 1)
            oy = Rot(self, st, "r1_y", [128, 512], BF16, 4)
            oz = Rot(self, st, "r1_z", [128, 512], F32, 4)

            def epi(nchunk, ti, ps, psB):
                t0, n = cfg.tiles[ti]
                if nchunk < RC:
                    o, oB = oy.next()
                    fw.op(fw.act, lambda: nc.scalar.activation(out=o[:, 0:n], in_=ps[0], func=AF.Gelu), reads=[psB[0]], writes=[oB])
                    fw.dma(self.Y[:, nchunk, t0:t0 + n], o[:, 0:n], reads=[oB])
                else:
                    o, oB = oz.next()
                    fw.op(fw.dve, lambda: nc.vector.tensor_copy(out=o[:, 0:n], in_=ps[0]), reads=[psB[0]], writes=[oB])
                    fw.dma(self.ZP[:, nchunk - RC, t0:t0 + n], o[:, 0:n], reads=[oB])

            self.linear(DC, [self.din["rec_w_in"][j]], 2 * cfg.Dr, 256, cfg.sups(2304), loader, epi)
        with ExitStack() as st:
            cw = self.sb(st, "cw", [128, 4, RC]); cwB = Buf()
            for tp in range(4):
                self.load_cols(cw[:, tp, :], cwB, self.din["rec_conv_w"][j, tp, :].rearrange("(n p) -> n p", p=128), RC)
            cb = self.sb(st, "cb", [128, RC]); cbB = Buf()
            self.load_cols(cb[:, :], cbB, self.din["rec_conv_b"][j, :].rearrange("(n p) -> n p", p=128), RC)
            gb = self.sb(st, "gb", [128, 2, 2, RC]); gbB = Buf()
            cl = self.sb(st, "cl", [128, 2, RC]); clB = Buf()
            for d in range(2):
                self.load_cols(gb[:, d, 0, :], gbB, self.din["rec_gate_a_b"][j, d, :].rearrange("(n p) -> n p", p=128), RC)
                self.load_cols(gb[:, d, 1, :], gbB, self.din["rec_gate_x_b"][j, d, :].rearrange("(n p) -> n p", p=128), RC)
                self.load_cols(cl[:, d, :], clB, self.din["rec_lambda"][j, d, :].rearrange("(n p) -> n p", p=128), RC)
            fw.op(fw.act, lambda: nc.scalar.activation(out=cl[:], in_=cl[:], func=AF.Exp, scale=-1.0), reads=[clB], writes=[clB])
            fw.op(fw.act, lambda: nc.scalar.activation(out=cl[:], in_=cl[:], func=AF.Ln, bias=1.0, scale=1.0), reads=[clB], writes=[clB])
            fw.op(fw.act, lambda: nc.scalar.mul(out=cl[:], in_=cl[:], mul=-RG_C), reads=[clB], writes=[clB])
            fw.barrier()
            with ExitStack() as s2:
                zpr = Rot(self, s2, "zp", [128, T], F32, 3)
                zr = Rot(self, s2, "z", [128, T], F32, 2)
                zbr = Rot(self, s2, "zb", [128, T], BF16, 2)
                zps = {}

                def ldz(c_):
                    zps[c_] = zpr.next()
                    fw.dma(zps[c_][0][:], self.ZP[:, c_, :], writes=[zps[c_][1]])

                ldz(0)
                for c in range(RC):
                    if c + 1 < RC:
                        ldz(c + 1)
                    zp, pB = zps.pop(c)
                    z, zB = zr.next()
                    for (s0, s1) in ((0, C), (C, T)):
                        fw.op(fw.act, lambda: nc.scalar.activation(out=z[:, s0:s1], in_=zp[:, s0:s1], func=AF.Identity,
                                                                   scale=cw[:, 2, c:c + 1], bias=cb[:, c:c + 1]),
                              reads=[pB, cwB, cbB], writes=[zB])
                        for (tp, do, so, ln) in ((0, 2, 0, s1 - s0 - 2), (1, 1, 0, s1 - s0 - 1), (3, 0, 1, s1 - s0 - 1)):
                            fw.op(fw.dve, lambda: nc.vector.scalar_tensor_tensor(
                                out=z[:, s0 + do:s0 + do + ln], in0=zp[:, s0 + so:s0 + so + ln], scalar=cw[:, tp, c:c + 1],
                                in1=z[:, s0 + do:s0 + do + ln], op0=ALU.mult, op1=ALU.add),
                                reads=[pB, cwB], writes=[zB])
                    zb, bB = zbr.next()
                    fw.op(fw.pool, lambda: nc.gpsimd.tensor_copy(out=zb[:], in_=z[:]), reads=[zB], writes=[bB])
                    fw.dma(self.Z[:, c, :], z[:], reads=[zB])
                    fw.dma(self.ZB[:, c, :], zb[:], reads=[bB])
                fw.barrier()
            with ExitStack() as s2:
                zt = self.sb(s2, "g_z", [128, T]); zB = Buf()
                A = [self.sb(s2, "g_a", [128, T]) for _ in range(2)]; AB = [Buf(), Buf()]
                Bt = [self.sb(s2, "g_b", [128, T]) for _ in range(2)]; BB = [Buf(), Buf()]
                tmp = self.sb(s2, "g_t", [128, T]); tB = Buf()
                NKMAX = 4
                zbk = [self.sb(s2, "g_zb", [128, T], BF16) for _ in range(NKMAX)]; zbkB = [Buf() for _ in range(NKMAX)]
                ytr = Rot(self, s2, "g_y", [128, T], BF16, 2)
                mt = self.sb(s2, "g_m", [128, T], BF16); mB = Buf()
                wgs = Rot(self, s2, "g_ws", [128, 4 * NKMAX, 128], F32, 2)
                wgb = Rot(self, s2, "g_wb", [128, 4 * NKMAX, 128], BF16, 2)
                gws = (self.din["rec_gate_a_w"], self.din["rec_gate_x_w"])

                def prefetch(c):
                    h0, h1 = (128 * c) // BW, (128 * c + 127) // BW
                    k0, k1 = (h0 * BW) // 128, ((h1 + 1) * BW - 1) // 128
                    ks = list(range(k0, k1 + 1))
                    assert len(ks) <= NKMAX
                    ws, wsB = wgs.next()
                    wb, wbB = wgb.next()
                    fw.op(fw.pool, lambda: nc.gpsimd.memset(ws[:], 0.0), writes=[wsB])
                    for d in range(2):
                        for gt in range(2):
                            for ki, k in enumerate(ks):
                                tix = (d * 2 + gt) * NKMAX + ki
                                for h in range(h0, h1 + 1):
                                    rl, rh = max(h * BW, 128 * k), min((h + 1) * BW, 128 * k + 128)
                                    c_l, c_h = max(h * BW, 128 * c), min((h + 1) * BW, 128 * c + 128)
                                    if rl >= rh or c_l >= c_h:
                                        continue
                                    fw.dma(ws[rl - 128 * k:rh - 128 * k, tix, c_l - 128 * c:c_h - 128 * c],
                                           gws[gt][j, d, h, rl - h * BW:rh - h * BW, c_l - h * BW:c_h - h * BW],
                                           writes=[Buf()], reads=[wsB])
                    fw.op(fw.pool, lambda: nc.gpsimd.tensor_copy(out=wb[:], in_=ws[:]), reads=[], writes=[wsB, wbB])
                    for ki, k in enumerate(ks):
                        fw.dma(zbk[ki][:], self.ZB[:, k, :], writes=[zbkB[ki]])
                    yt, yB = ytr.next()
                    fw.dma(yt[:], self.Y[:, c, :], writes=[yB])
                    return ks, wb, wbB, yt, yB

                nxt = prefetch(0)
                for c in range(RC):
                    ks, wb, wbB, yt, yB = nxt
                    fw.dma(zt[:], self.Z[:, c, :], writes=[zB])
                    for tix_, (t0, n) in enumerate(cfg.tiles):
                        for d in range(2):
                            banks = []
                            for gt in range(2):
                                b = self.next_bank()
                                banks.append(b)
                                for ki, k in enumerate(ks):
                                    tix = (d * 2 + gt) * NKMAX + ki
                                    fw.op(fw.pe, lambda: nc.tensor.matmul(self.PS[b][:, 0:n], lhsT=wb[:, tix, :], rhs=zbk[ki][:, t0:t0 + n],
                                                                          start=(ki == 0), stop=(ki == len(ks) - 1)),
                                          reads=[wbB, zbkB[ki]], writes=[self.PSB[b]])
                            fw.op(fw.act, lambda: nc.scalar.activation(out=A[d][:, t0:t0 + n], in_=self.PS[banks[0]][:, 0:n], func=AF.Sigmoid,
                                                                       bias=gb[:, d, 0, c:c + 1], scale=1.0),
                                  reads=[self.PSB[banks[0]], gbB], writes=[AB[d]])
                            fw.op(fw.act, lambda: nc.scalar.activation(out=Bt[d][:, t0:t0 + n], in_=self.PS[banks[1]][:, 0:n], func=AF.Sigmoid,
                                                                       bias=gb[:, d, 1, c:c + 1], scale=1.0),
                                  reads=[self.PSB[banks[1]], gbB], writes=[BB[d]])
                    if c + 1 < RC:
                        nxt = prefetch(c + 1)
                    for d in range(2):
                        fw.op(fw.pool, lambda: nc.gpsimd.tensor_tensor(out=Bt[d][:], in0=Bt[d][:], in1=zt[:], op=ALU.mult),
                              reads=[zB], writes=[BB[d]])
                    for d in range(2):
                        fw.op(fw.act, lambda: nc.scalar.activation(out=A[d][:], in_=A[d][:], func=AF.Exp, scale=cl[:, d, c:c + 1]),
                              reads=[clB], writes=[AB[d]])
                    for d, (sc_, sB_) in enumerate(((tmp, tB), (zt, zB))):
                        fw.op(fw.act, lambda: nc.scalar.activation(out=sc_[:], in_=A[d][:], func=AF.Square), reads=[AB[d]], writes=[sB_])
                        fw.op(fw.act, lambda: nc.scalar.activation(out=sc_[:], in_=sc_[:], func=AF.Sqrt, scale=-1.0, bias=1.0),
                              reads=[sB_], writes=[sB_])
                    fw.op(fw.dve, lambda: nc.vector.tensor_tensor(out=Bt[0][:], in0=Bt[0][:], in1=tmp[:], op=ALU.mult),
                          reads=[tB], writes=[BB[0]])
                    fw.op(fw.dve, lambda: nc.vector.tensor_tensor_scan(out=tmp[:, :], data0=A[0][:, :], data1=Bt[0][:, :], initial=0.0,
                                                                       op0=ALU.mult, op1=ALU.add),
                          reads=[AB[0], BB[0]], writes=[tB])
                    fw.op(fw.dve, lambda: nc.vector.tensor_tensor(out=Bt[1][:], in0=Bt[1][:], in1=zt[:], op=ALU.mult),
                          reads=[zB], writes=[BB[1]])
                    fw.op(fw.dve, lambda: nc.vector.tensor_tensor_scan(out=zt[:, 0:C][:, ::-1], data0=A[1][:, 0:C][:, ::-1],
                                                                       data1=Bt[1][:, 0:C][:, ::-1], initial=0.0,
                                                                       op0=ALU.mult, op1=ALU.add),
                          reads=[AB[1], BB[1]], writes=[zB])
                    fw.op(fw.dve, lambda: nc.vector.tensor_tensor_scan(out=zt[:, C:T][:, ::-1], data0=A[1][:, C:T][:, ::-1],
                                                                       data1=Bt[1][:, C:T][:, ::-1], initial=zt[:, 0:1],
                                                                       op0=ALU.mult, op1=ALU.add),
                          reads=[AB[1], BB[1]], writes=[zB])
                    fw.op(fw.dve, lambda: nc.vector.tensor_tensor(out=tmp[:], in0=tmp[:], in1=zt[:], op=ALU.add), reads=[zB], writes=[tB])
                    fw.op(fw.pool, lambda: nc.gpsimd.tensor_tensor(out=mt[:], in0=tmp[:], in1=yt[:], op=ALU.mult),
                          reads=[tB, yB], writes=[mB])
                    fw.dma(self.M[:, c, :], mt[:], reads=[mB])
                fw.barrier()
        with ExitStack() as st:
            epi, pre = self.mk_resid_epi(st, 2)
            self.linear(RC, [self.din["rec_w_out"][j]], cfg.D, 256, cfg.sups(2304), self.mk_direct_loader(self.M, RC), epi, pre=pre, nst=1)

    def att_mixer(self, j, lst, last):
        cfg, fw, nc = self.cfg, self.fw, self.nc
        DC, T, C, S, H, KV, G = cfg.DC, cfg.T, cfg.C, cfg.S, cfg.H, cfg.KV, cfg.G
        GW = G * 128
        with ExitStack() as ast:
            kT = self.sb(ast, "kT", [128, KV, T], BF16)
            kTB = Buf()
            V = self.sb(ast, "V", [128, T // 128, KV * 128], BF16)
            VB = Buf()
            with ExitStack() as st:
                loader = self.mk_mod_loader(st, 0, 1, div=4)
                sups = cfg.sups(1024)
                Tmax = max(sum(cfg.tiles[i][1] for i in sup) for sup in sups)
                cs = self.sb(st, "cos", [128, Tmax]); sn = self.sb(st, "sin", [128, Tmax]); csB = Buf()
                state = {}

                def sup_begin(sup):
                    off = 0
                    state["off"] = {}
                    for ti in sup:
                        t0, n = cfg.tiles[ti]
                        state["off"][ti] = off
                        if ti != 0:
                            fw.dma(cs[:, off:off + n], self.din["cos2"][:, t0 - C:t0 - C + n], writes=[csB])
                            fw.dma(sn[:, off:off + n], self.din["sinx"][:, t0 - C:t0 - C + n], writes=[csB])
                        off += n

                t1r = Rot(self, st, "rp1", [128, 512], F32, 3)
                t2r = Rot(self, st, "rp2", [128, 512], F32, 3)
                qo = Rot(self, st, "qo", [128, 512], BF16, 4)
                NGQ = 256
                gv0 = (H + KV) * 128 // NGQ

                def epi(nchunk, ti, ps, psB):
                    t0, n = cfg.tiles[ti]
                    if isinstance(nchunk, tuple):
                        _, g, blk = nchunk
                        col0 = (g - gv0) * NGQ
                        fw.op(fw.act, lambda: nc.scalar.copy(out=V[:, t0 // 128 + blk, col0:col0 + NGQ], in_=ps[0]),
                              reads=[psB[0]], writes=[VB])
                        return
                    isq = nchunk < H
                    if isq:
                        dst, dB = qo.next()
                        dst = dst[:, 0:n]
                    else:
                        dst, dB = kT[:, nchunk - H, t0:t0 + n], kTB
                    if ti == 0:
                        fw.op(fw.act, lambda: nc.scalar.copy(out=dst, in_=ps[0]), reads=[psB[0]], writes=[dB])
                    else:
                        o = state["off"][ti]
                        t1, t1B = t1r.next()
                        t2, t2B = t2r.next()
                        fw.op(fw.dve, lambda: nc.vector.tensor_tensor(out=t1[:, 0:n], in0=ps[0], in1=cs[:, o:o + n], op=ALU.mult),
                              reads=[psB[0], csB], writes=[t1B])
                        fw.op(fw.dve, lambda: nc.vector.tensor_tensor(out=t2[0:64, 0:n], in0=ps[0][64:128, :], in1=sn[64:128, o:o + n], op=ALU.mult),
                              reads=[psB[0], csB], writes=[t2B])
                        fw.op(fw.dve, lambda: nc.vector.tensor_tensor(out=t2[64:128, 0:n], in0=ps[0][0:64, :], in1=sn[0:64, o:o + n], op=ALU.mult),
                              reads=[psB[0], csB], writes=[t2B])
                        fw.op(fw.pool, lambda: nc.gpsimd.tensor_tensor(out=dst, in0=t1[:, 0:n], in1=t2[:, 0:n], op=ALU.add),
                              reads=[t1B, t2B], writes=[dB])
                    if isq:
                        fw.dma(self.Q[:, nchunk, t0:t0 + n], dst, reads=[dB])

                self.linear(DC, [self.din["att_w_qkv"][j]], cfg.NQKV, NGQ, sups, loader, epi,
                            modeA=lambda g: g >= gv0, sup_begin=sup_begin)
            with ExitStack() as st:
                es = self.sb(st, "esink", [128, H]); esB = Buf()
                fw.dma(es[:], self.din["att_sink"][j:j + 1, :].to_broadcast([128, H]), writes=[esB])
                fw.op(fw.act, lambda: nc.scalar.activation(out=es[:], in_=es[:], func=AF.Exp), reads=[esB], writes=[esB])
                esf = self.sb(st, "esf", [128, H, 128]); esfB = Buf()
                fw.op(fw.pool, lambda: nc.gpsimd.memset(esf[:], 0.0), writes=[esfB])
                for h in range(H):
                    fw.op(fw.pool, lambda: nc.gpsimd.tensor_scalar(out=esf[:, h, :], in0=esf[:, h, :], scalar1=es[:, h:h + 1], scalar2=None,
                                                                   op0=ALU.add), reads=[esB], writes=[esfB])
                mk32 = self.sb(st, "mk32", [128, 2, 128]); mk = self.sb(st, "mk", [128, 2, G, 128], BF16); mkB = Buf()
                fw.dma(mk32[:, 0, :], self.din["mask_prev"][:, :], writes=[mkB])
                fw.dma(mk32[:, 1, :], self.din["mask_next"][:, :], writes=[mkB])
                fw.op(fw.pool, lambda: nc.gpsimd.tensor_scalar(out=mk32[:], in0=mk32[:], scalar1=-1.0, scalar2=30000.0,
                                                               op0=ALU.add, op1=ALU.mult), reads=[mkB], writes=[mkB])
                for m in range(2):
                    for g in range(G):
                        fw.op(fw.pool, lambda: nc.gpsimd.tensor_copy(out=mk[:, m, g, :], in_=mk32[:, m, :]), reads=[mkB], writes=[mkB])
                idb = self.sb(st, "identb", [128, 128], BF16); idbB = Buf()
                fw.op(fw.pool, lambda: nc.gpsimd.tensor_copy(out=idb[:], in_=self.ident[:]), reads=[self.identB], writes=[idbB])
                qr = Rot(self, st, "aq", [128, H, 128], BF16, 4)
                pr = Rot(self, st, "ap", [128, GW], BF16, 5)
                orr = Rot(self, st, "ao", [128, H, 128], BF16, 2)
                dr = Rot(self, st, "ad", [128, GW], F32, 2)
                NB = S // 128
                blocks = []
                if True:
                    for qb in range(C // 128):
                        blocks.append((qb * 128, [(kb * 128, None) for kb in range(C // 128)]))
                for jb in range(NB):
                    ch = []
                    for o_, mt_ in ((-1, 0), (0, None), (1, 1)):
                        if 0 <= jb + o_ < NB:
                            ch.append((C + (jb + o_) * 128, mt_))
                    ch += [(kb * 128, None) for kb in range(C // 128)]
                    blocks.append((C + jb * 128, ch))
                units = []
                for bi, (qoff, ch) in enumerate(blocks):
                    for kh in range(KV):
                        for ci, (koff, mt_) in enumerate(ch):
                            units.append((bi, kh, ci, koff, mt_, len(ch)))
                qtiles = {}

                def ensure_q(bi):
                    if bi < len(blocks) and bi not in qtiles:
                        qt, qB = qr.next()
                        qoff = blocks[bi][0]
                        fw.dma(qt[:], self.Q[:, :, qoff:qoff + 128], writes=[qB])
                        qtiles[bi] = (qt, qB)

                sc = cfg.HD ** -0.5
                SBANK = [0, 1, 2, 3]
                pts = {}

                def emitS(ui):
                    bi, kh, ci, koff, mt_, nch = units[ui]
                    ensure_q(bi)
                    ensure_q(bi + 1)
                    qt, qB = qtiles[bi]
                    b = SBANK[ui % 4]
                    fw.op(fw.pe, lambda: nc.tensor.matmul(self.PS[b][:, 0:GW], lhsT=kT[:, kh, koff:koff + 128],
                                                          rhs=qt[:, kh * G:(kh + 1) * G, :].rearrange("p a b -> p (a b)"),
                                                          start=True, stop=(mt_ is None)),
                          reads=[kTB, qB], writes=[self.PSB[b]])
                    if mt_ is not None:
                        fw.op(fw.pe, lambda: nc.tensor.matmul(self.PS[b][:, 0:GW], lhsT=idb[:], rhs=mk[:, mt_, :, :].rearrange("p a b -> p (a b)"),
                                                              start=False, stop=True),
                              reads=[mkB, idbB], writes=[self.PSB[b]])

                cur = {}

                def emitPV(ui):
                    bi, kh, ci, koff, mt_, nch = units[ui]
                    b = SBANK[ui % 4]
                    pt, pB = pr.next()
                    fw.op(fw.act, lambda: nc.scalar.activation(out=pt[:], in_=self.PS[b][:, 0:GW], func=AF.Exp, scale=sc),
                          reads=[self.PSB[b]], writes=[pB])
                    pair = bi * KV + kh
                    ob, db = 4 + pair % 2, 6 + pair % 2
                    fw.op(fw.pe, lambda: nc.tensor.matmul(self.PS[ob][:, 0:GW], lhsT=V[:, koff // 128, kh * 128:(kh + 1) * 128], rhs=pt[:],
                                                          start=(ci == 0), stop=(ci == nch - 1)),
                          reads=[VB, pB], writes=[self.PSB[ob]])
                    fw.op(fw.pe, lambda: nc.tensor.matmul(self.PS[db][:, 0:GW], lhsT=self.ones_b[:], rhs=pt[:],
                                                          start=(ci == 0), stop=(ci == nch - 1)),
                          reads=[self.onesB, pB], writes=[self.PSB[db]])
                    if ci == nch - 1:
                        if kh == 0:
                            cur["o"] = orr.next()
                        ot, oB = cur["o"]
                        dt_, dB = dr.next()
                        fw.op(fw.dve, lambda: nc.vector.tensor_tensor(out=dt_[:], in0=self.PS[db][:, 0:GW],
                                                                      in1=esf[:, kh * G:(kh + 1) * G, :].rearrange("p a b -> p (a b)"), op=ALU.add),
                              reads=[self.PSB[db], esfB], writes=[dB])
                        fw.op(fw.dve, lambda: nc.vector.reciprocal(out=dt_[:], in_=dt_[:]), reads=[dB], writes=[dB])
                        fw.op(fw.dve, lambda: nc.vector.tensor_tensor(out=ot[:, kh * G:(kh + 1) * G, :].rearrange("p a b -> p (a b)"),
                                                                      in0=self.PS[ob][:, 0:GW], in1=dt_[:], op=ALU.mult),
                              reads=[self.PSB[ob], dB], writes=[oB])
                        if kh == KV - 1:
                            qoff = blocks[bi][0]
                            fw.dma(self.O[:, :, qoff:qoff + 128], ot[:], reads=[oB])
                            qtiles.pop(bi, None)

                emitS(0)
                if len(units) > 1:
                    emitS(1)
                for ui in range(len(units)):
                    if ui + 2 < len(units):
                        emitS(ui + 2)
                    emitPV(ui)
                fw.barrier()
        with ExitStack() as st:
            epi, pre = self.mk_resid_epi(st, 2)
            self.linear(DC, [self.din["att_w_o"][j]], cfg.D, 256, cfg.sups(2304), self.mk_direct_loader(self.O, DC), epi, pre=pre)


def input_shapes(cfg):
    D, S, C, Dr, F, L = cfg.D, cfg.S, cfg.C, cfg.Dr, cfg.F, cfg.depth
    return {
        "x": (S, D), "ctx": (C, D), "c": (1, D), "c_ctx": (1, D),
        "mod_w": (L, D, 6 * D), "mod_b": (L, 6 * D),
        "ln_mix_g": (L, D), "ln_mix_b": (L, D), "ln_ffn_g": (L, D), "ln_ffn_b": (L, D),
        "ffn_w_gate": (L, D, F), "ffn_w_up": (L, D, F), "ffn_w_down": (L, F, D),
        "rec_w_in": (cfg.n_rec, D, 2 * Dr), "rec_conv_w": (cfg.n_rec, 4, Dr), "rec_conv_b": (cfg.n_rec, Dr),
        "rec_gate_a_w": (cfg.n_rec, 2, cfg.NBLK, cfg.BW, cfg.BW), "rec_gate_a_b": (cfg.n_rec, 2, Dr),
        "rec_gate_x_w": (cfg.n_rec, 2, cfg.NBLK, cfg.BW, cfg.BW), "rec_gate_x_b": (cfg.n_rec, 2, Dr),
        "rec_lambda": (cfg.n_rec, 2, Dr), "rec_w_out": (cfg.n_rec, Dr, D),
        "att_w_qkv": (max(cfg.n_att, 1), D, cfg.NQKV), "att_sink": (max(cfg.n_att, 1), cfg.H),
        "att_w_o": (max(cfg.n_att, 1), D, D),
        "ident": (128, 128), "cos2": (128, S), "sinx": (128, S), "mask_prev": (128, 128), "mask_next": (128, 128),
    }


def constants(cfg):
    S = cfg.S
    t = np.arange(S)
    row = (t // cfg.grid_w).astype(np.float32)
    col = (t % cfg.grid_w).astype(np.float32)
    axis_dim = cfg.HD // 2
    inv = (cfg.rope_base ** (-np.arange(0, axis_dim, 2, dtype=np.float32) / axis_dim)).astype(np.float32)
    ang = np.concatenate([row[:, None] * inv, col[:, None] * inv], axis=-1).astype(np.float32)
    cos, sin = np.cos(ang).T.astype(np.float32), np.sin(ang).T.astype(np.float32)
    p = np.arange(128)[:, None]
    i = np.arange(128)[None, :]
    return {
        "ident": np.eye(128, dtype=np.float32),
        "cos2": np.ascontiguousarray(np.concatenate([cos, cos], 0)),
        "sinx": np.ascontiguousarray(np.concatenate([sin, -sin], 0)),
        "mask_prev": (p >= i).astype(np.float32),
        "mask_next": (p <= i).astype(np.float32),
    }


_CACHE = {}


def run(cfg, inputs, n_cores):
    key = (cfg.D, cfg.S, cfg.depth)
    if key not in _CACHE:
        _CACHE[key] = KB(cfg).build()
    nc = _CACHE[key]
    consts = constants(cfg)
    f = lambda a: np.ascontiguousarray(np.asarray(a, dtype=np.float32))
    shared = {k: f(v) for k, v in inputs.items() if k not in ("x", "c", "ctx", "c_ctx")}
    shared.update(consts)
    shared["c_ctx"] = f(inputs["c_ctx"]).reshape(1, -1)
    in_maps = []
    for b in range(n_cores):
        m = dict(shared)
        m["x"] = f(inputs["x"][b])
        m["ctx"] = f(inputs["ctx"][b])
        m["c"] = f(inputs["c"][b:b + 1])
        in_maps.append(m)
    res = run_bass_kernel_spmd(nc, in_maps, core_ids=list(range(n_cores)))
    return np.stack([np.asarray(r["out"], dtype=np.float32) for r in res.results], 0)


def kernel(**inputs):
    cfg = Cfg()
    B = inputs["x"].shape[0]
    return run(cfg, inputs, B)
```

```python
import math
from contextlib import ExitStack

import numpy as np
import concourse.bass as bass
import concourse.mybir as mybir
from concourse.bass_utils import run_bass_kernel_spmd

F32 = mybir.dt.float32
BF16 = mybir.dt.bfloat16
ALU = mybir.AluOpType
AF = mybir.ActivationFunctionType
LN_EPS = 1e-5
LN_FP32R = False
STORES_ON_ACT = False
RG_C = 8.0


class Cfg:
    def __init__(self, D=2048, S=4096, C=256, Dr=2816, NBLK=16, F=5632, HD=128, KV=4,
                 depth=4, grid_w=64, rope_base=10000.0):
        self.D, self.S, self.C, self.Dr, self.NBLK, self.F = D, S, C, Dr, NBLK, F
        self.HD, self.KV, self.depth = HD, KV, depth
        self.H = D // HD
        self.G = self.H // KV
        self.BW = Dr // NBLK
        self.DC, self.RC, self.FC = D // 128, Dr // 128, F // 128
        self.T = C + S
        self.NQKV = (self.H + 2 * KV) * HD
        self.grid_w, self.rope_base = grid_w, rope_base
        self.alpha = (2.0 * depth) ** 0.25
        self.n_rec = (depth + 1) // 2
        self.n_att = depth // 2
        self.tiles = [(0, C)] + [(C + 512 * j, 512) for j in range(S // 512)]

    def sups(self, maxtok):
        out, cur, n = [], [], 0
        for i, (_, t) in enumerate(self.tiles):
            if cur and n + t > maxtok:
                out.append(cur)
                cur, n = [], 0
            cur.append(i)
            n += t
        if cur:
            out.append(cur)
        return out


class Buf:
    __slots__ = ("w", "r")

    def __init__(self):
        self.w = {}
        self.r = {}


class Eng:
    def __init__(self, e, sem, sid, name):
        self.e, self.sem, self.sid, self.name = e, sem, sid, name
        self.count = 0
        self.waited = {}


class FW:
    def __init__(self, nc, stack, n_dma_sems=24):
        self.nc = nc
        self.sems = []

        def mk(name):
            s = stack.enter_context(nc.semaphore(name))
            self.sems.append(s)
            return s, len(self.sems) - 1

        self.pe = Eng(nc.tensor, *mk("c_pe"), "pe")
        self.act = Eng(nc.scalar, *mk("c_act"), "act")
        self.dve = Eng(nc.vector, *mk("c_dve"), "dve")
        self.pool = Eng(nc.gpsimd, *mk("c_pool"), "pool")
        self.sp = Eng(nc.sync, None, None, "sp")
        self.engs = [self.pe, self.act, self.dve, self.pool, self.sp]
        self.dsems = []
        for i in range(n_dma_sems):
            s, sid = mk("d%d" % i)
            self.dsems.append([s, sid, 0])
        self.dnext = 0
        self.bar_sem, self.bar_sid = mk("bar")
        self.bar_cnt = 0
        self.bar_src = self.bar_dst = None

    def _wait(self, eng, sid, val):
        if val <= 0 or eng.waited.get(sid, 0) >= val:
            return
        eng.e.wait_ge(self.sems[sid], val)
        eng.waited[sid] = val

    def _deps(self, eng, reads, writes):
        deps = {}
        for b in reads:
            for s, v in b.w.items():
                if deps.get(s, 0) < v:
                    deps[s] = v
        for b in writes:
            for s, v in b.w.items():
                if deps.get(s, 0) < v:
                    deps[s] = v
            for s, v in b.r.items():
                if deps.get(s, 0) < v:
                    deps[s] = v
        for s, v in deps.items():
            if eng is self.pe and s == self.pe.sid:
                continue
            self._wait(eng, s, v)

    def _mark(self, sid, val, reads, writes):
        for b in reads:
            if b.r.get(sid, 0) < val:
                b.r[sid] = val
        for b in writes:
            b.w = {sid: val}
            b.r = {}

    def op(self, eng, fn, reads=(), writes=()):
        self._deps(eng, reads, writes)
        ins = fn()
        eng.count += 1
        ins.then_inc(eng.sem, 1)
        self._mark(eng.sid, eng.count, reads, writes)
        return ins

    def dma(self, out, in_, reads=(), writes=(), **kw):
        q = self.act if (STORES_ON_ACT and str(out.space).endswith("DRAM")) else self.sp
        d = self.dsems[self.dnext]
        self.dnext = (self.dnext + 1) % len(self.dsems)
        self._wait(q, d[1], d[2])
        self._deps(q, reads, writes)
        ins = q.e.dma_start(out=out, in_=in_, **kw)
        d[2] += 16
        ins.then_inc(d[0], 16)
        self._mark(d[1], d[2], reads, writes)
        return ins

    def barrier(self):
        sp = self.sp
        for e in (self.pe, self.act, self.dve, self.pool):
            self._wait(sp, e.sid, e.count)
        for d in self.dsems:
            self._wait(sp, d[1], d[2])
        ins = sp.e.dma_start(out=self.bar_dst, in_=self.bar_src)
        self.bar_cnt += 16
        ins.then_inc(self.bar_sem, 16)
        for e in self.engs:
            self._wait(e, self.bar_sid, self.bar_cnt)


class Rot:
    def __init__(self, kb, st, name, shape, dt, count):
        self.items = [(st.enter_context(kb.nc.sbuf_tensor("%s_%d_%d" % (name, kb.uid(), i), shape, dt)), Buf())
                      for i in range(count)]
        self.i = 0

    def next(self):
        it = self.items[self.i]
        self.i = (self.i + 1) % len(self.items)
        return it


class KB:
    def __init__(self, cfg):
        self.cfg = cfg
        self._uid = 0
        self.nc = bass.Bass("TRN2", target_bir_lowering=False)

    def uid(self):
        self._uid += 1
        return self._uid

    def sb(self, st, name, shape, dt=F32):
        return st.enter_context(self.nc.sbuf_tensor("%s_%d" % (name, self.uid()), shape, dt))

    def build(self):
        cfg, nc = self.cfg, self.nc
        D, S, C, T, Dr, F, L = cfg.D, cfg.S, cfg.C, cfg.T, cfg.Dr, cfg.F, cfg.depth
        self.din = {}

        def I(name, shape):
            self.din[name] = nc.dram_tensor(name, list(shape), F32, kind="ExternalInput").ap()

        for name, shape in input_shapes(cfg).items():
            I(name, shape)
        self.out = nc.dram_tensor("out", [S, D], F32, kind="ExternalOutput").ap()

        def scratch(name, rows, dt):
            return nc.dram_tensor(name, [rows, T], dt).ap().rearrange("(c p) t -> p c t", p=128)

        self.HA = scratch("s_ha", D, F32)
        self.R = scratch("s_r", D, F32)
        self.Y = scratch("s_y", Dr, BF16)
        self.ZP = scratch("s_zp", Dr, F32)
        self.Z = scratch("s_z", Dr, F32)
        self.ZB = scratch("s_zb", Dr, BF16)
        self.M = scratch("s_m", Dr, BF16)
        self.Q = scratch("s_q", D, BF16)
        self.O = scratch("s_o", D, BF16)
        self.AFF = scratch("s_af", F, BF16)
        self.KD = scratch("s_k", cfg.KV * 128, BF16)
        self.VD = nc.dram_tensor("s_v", [128, (T // 128) * cfg.KV * 128], BF16).ap().rearrange("p (b n) -> p b n", n=cfg.KV * 128)

        with ExitStack() as st:
            self.fw = fw = FW(nc, st)
            b0 = self.sb(st, "bar0", [1, 8])
            b1 = self.sb(st, "bar1", [1, 8])
            fw.bar_src, fw.bar_dst = b0[:], b1[:]
            self.PS = [st.enter_context(nc.psum_tensor("ps%d" % i, [128, 512], F32)) for i in range(8)]
            self.PSB = [Buf() for _ in range(8)]
            self.bank_i = 0
            self.ident = self.sb(st, "ident", [128, 128])
            self.identB = Buf()
            fw.dma(self.ident[:], self.din["ident"][:, :], writes=[self.identB])
            self.ones_f = self.sb(st, "ones_f", [128, 128])
            self.ones_b = self.sb(st, "ones_b", [128, 128], BF16)
            self.onesB = Buf()
            fw.op(fw.pool, lambda: nc.gpsimd.memset(self.ones_f[:], 1.0), writes=[self.onesB])
            fw.op(fw.pool, lambda: nc.gpsimd.memset(self.ones_b[:], 1.0), writes=[self.onesB])
            self.colst = self.sb(st, "colst", [128, 128])
            self.colstB = Buf()
            self.silc = self.sb(st, "silc", [128, cfg.DC, 2])
            self.silcB = Buf()
            tmpc = self.sb(st, "tmpc", [128, cfg.DC])
            tmpcB = Buf()
            for j, nm in enumerate(("c", "c_ctx")):
                self.load_cols(tmpc[:, :], tmpcB, self.din[nm][0, :].rearrange("(n p) -> n p", p=128), cfg.DC)
                fw.op(fw.act, lambda: nc.scalar.activation(out=self.silc[:, :, j], in_=tmpc[:, :], func=AF.Silu),
                      reads=[tmpcB], writes=[self.silcB])
            fw.barrier()
            self.phase_in()
            for l in range(L):
                with ExitStack() as lst:
                    self.layer(l, lst)
            self.phase_out()
            fw.barrier()
        return nc

    def next_bank(self):
        b = self.bank_i
        self.bank_i = (self.bank_i + 1) % 8
        return b

    def load_cols(self, dst, dstB, src2d, n, scale=None):
        fw, nc = self.fw, self.nc
        fw.dma(self.colst[0:n, :], src2d, writes=[self.colstB])
        b = self.next_bank()
        fw.op(fw.pe, lambda: nc.tensor.transpose(self.PS[b][:, 0:n], self.colst[0:n, :], self.ident[0:n, 0:n]),
              reads=[self.colstB, self.identB], writes=[self.PSB[b]])
        if scale is None:
            fw.op(fw.act, lambda: nc.scalar.copy(out=dst, in_=self.PS[b][:, 0:n]), reads=[self.PSB[b]], writes=[dstB])
        else:
            fw.op(fw.act, lambda: nc.scalar.mul(out=dst, in_=self.PS[b][:, 0:n], mul=scale),
                  reads=[self.PSB[b]], writes=[dstB])

    def phase_in(self):
        cfg, fw, nc = self.cfg, self.fw, self.nc
        DC = cfg.DC
        with ExitStack() as st:
            xin = Rot(self, st, "xin", [128, cfg.D], F32, 3)
            stg = Rot(self, st, "xstg", [128, DC, 128], F32, 2)
            blocks = [("ctx", r, r) for r in range(0, cfg.C, 128)] + [("x", r, cfg.C + r) for r in range(0, cfg.S, 128)]
            xs_ = {}

            def ldx(i_):
                nm_, r0_, _ = blocks[i_]
                xs_[i_] = xin.next()
                fw.dma(xs_[i_][0][:], self.din[nm_][r0_:r0_ + 128, :], writes=[xs_[i_][1]])

            ldx(0)
            for bi_, (nm, r0, t0) in enumerate(blocks):
                if bi_ + 1 < len(blocks):
                    ldx(bi_ + 1)
                xt, xB = xs_.pop(bi_)
                sg, sB = stg.next()
                for g0 in range(0, DC, 4):
                    gn = min(4, DC - g0)
                    b = self.next_bank()
                    for i in range(gn):
                        fw.op(fw.pe, lambda: nc.tensor.transpose(self.PS[b][:, i * 128:(i + 1) * 128],
                                                                  xt[:, (g0 + i) * 128:(g0 + i + 1) * 128], self.ident[:]),
                              reads=[xB, self.identB], writes=[self.PSB[b]])
                    fw.op(fw.act, lambda: nc.scalar.mul(out=sg[:, g0:g0 + gn, :].rearrange("p a b -> p (a b)"),
                                                        in_=self.PS[b][:, 0:gn * 128], mul=cfg.alpha),
                          reads=[self.PSB[b]], writes=[sB])
                fw.dma(self.HA[:, :, t0:t0 + 128], sg[:], reads=[sB])
            fw.barrier()

    def phase_out(self):
        cfg, fw, nc = self.cfg, self.fw, self.nc
        DC = cfg.DC
        with ExitStack() as st:
            hin = Rot(self, st, "hin", [128, DC, 128], F32, 3)
            ost = Rot(self, st, "ost", [128, cfg.D], F32, 2)
            hs_ = {}

            def ldh(r_):
                hs_[r_] = hin.next()
                fw.dma(hs_[r_][0][:], self.HA[:, :, cfg.C + r_:cfg.C + r_ + 128], writes=[hs_[r_][1]])

            ldh(0)
            for r0 in range(0, cfg.S, 128):
                t0 = cfg.C + r0
                if r0 + 128 < cfg.S:
                    ldh(r0 + 128)
                ht, hB = hs_.pop(r0)
                ot, oB = ost.next()
                for g0 in range(0, DC, 4):
                    gn = min(4, DC - g0)
                    b = self.next_bank()
                    for i in range(gn):
                        fw.op(fw.pe, lambda: nc.tensor.transpose(self.PS[b][:, i * 128:(i + 1) * 128],
                                                                  ht[:, g0 + i, :], self.ident[:]),
                              reads=[hB, self.identB], writes=[self.PSB[b]])
                    fw.op(fw.act, lambda: nc.scalar.copy(out=ot[:, g0 * 128:(g0 + gn) * 128],
                                                         in_=self.PS[b][:, 0:gn * 128]),
                          reads=[self.PSB[b]], writes=[oB])
                fw.dma(self.out[r0:r0 + 128, :], ot[:], reads=[oB])

    def layer(self, l, lst):
        cfg, fw, nc = self.cfg, self.fw, self.nc
        DC = cfg.DC
        last = l == cfg.depth - 1
        j = l // 2
        is_rec = (l % 2 == 0)
        self.modv = self.sb(lst, "modv", [128, 6 * DC, 2])
        self.modvB = Buf()
        self.phase_mod(l)
        self.ln = self.sb(lst, "lncols", [128, 4, DC])
        self.lnB = Buf()
        a2 = 1.0 if last else cfg.alpha
        for i, (nm, sc) in enumerate((("ln_mix_g", cfg.alpha), ("ln_mix_b", cfg.alpha), ("ln_ffn_g", a2), ("ln_ffn_b", a2))):
            self.load_cols(self.ln[:, i, :], self.lnB, self.din[nm][l, :].rearrange("(n p) -> n p", p=128), DC, scale=sc)
        fw.barrier()
        if is_rec:
            self.rec_mixer(j, lst)
        else:
            self.att_mixer(j, lst, last)
        self.phase_ln(0)
        self.ffn(l)
        self.phase_ln(1)

    def mcol(self, grp, kc, ti):
        return self.modv[:, grp * self.cfg.DC + kc, (1 if ti == 0 else 0):(2 if ti == 0 else 1)]

    def phase_mod(self, l):
        cfg, fw, nc = self.cfg, self.fw, self.nc
        DC = cfg.DC
        N = 6 * cfg.D
        NG = 512
        wv = self.din["mod_w"][l].rearrange("(kc p) n -> p kc n", p=128)
        with ExitStack() as st:
            wsl = Rot(self, st, "modw", [128, DC, NG], F32, 2)
            mb = self.sb(st, "modb", [128, 6 * DC])
            mbB = Buf()
            self.load_cols(mb[:, :], mbB, self.din["mod_b"][l, :].rearrange("(n p) -> n p", p=128), 6 * DC)
            rowb = self.sb(st, "modrow", [2, N])
            rowB = Buf()
            nxt = wsl.next()
            fw.dma(nxt[0][:], wv[:, :, 0:NG], writes=[nxt[1]])
            for g in range(N // NG):
                wt, wB = nxt
                if g + 1 < N // NG:
                    nxt = wsl.next()
                    fw.dma(nxt[0][:], wv[:, :, (g + 1) * NG:(g + 2) * NG], writes=[nxt[1]])
                b = self.next_bank()
                for kc in range(DC):
                    fw.op(fw.pe, lambda: nc.tensor.matmul(self.PS[b][0:2, 0:NG], lhsT=self.silc[:, kc, :], rhs=wt[:, kc, :],
                                                          start=(kc == 0), stop=(kc == DC - 1)),
                          reads=[wB, self.silcB], writes=[self.PSB[b]])
                fw.op(fw.act, lambda: nc.scalar.copy(out=rowb[0:2, g * NG:(g + 1) * NG], in_=self.PS[b][0:2, 0:NG]),
                      reads=[self.PSB[b]], writes=[rowB])
            pb = self.next_bank()
            psm = self.PS[pb]
            for ch in range(6 * DC):
                fw.op(fw.pe, lambda: nc.tensor.transpose(psm[:, 2 * ch:2 * ch + 2], rowb[0:2, ch * 128:(ch + 1) * 128], self.ident[0:2, 0:2]),
                      reads=[rowB, self.identB], writes=[self.PSB[pb]])
            for jx in range(2):
                fw.op(fw.dve, lambda: nc.vector.tensor_tensor(
                    out=self.modv[:, :, jx], in0=psm[:, 0:12 * DC].rearrange("p (a b) -> p a b", b=2)[:, :, jx],
                    in1=mb[:, :], op=ALU.add), reads=[self.PSB[pb], mbB], writes=[self.modvB])
            for grp in (1, 4):
                fw.op(fw.dve, lambda: nc.vector.tensor_scalar(
                    out=self.modv[:, grp * DC:(grp + 1) * DC, :], in0=self.modv[:, grp * DC:(grp + 1) * DC, :],
                    scalar1=1.0, scalar2=1.0 / cfg.alpha, op0=ALU.add, op1=ALU.mult),
                    reads=[self.modvB], writes=[self.modvB])
            fw.barrier()

    def linear(self, KC, wlist, N, NG, sups, load_act, epi, modeA=None, sup_begin=None, pre=None, nst=2):
        cfg, fw, nc = self.cfg, self.fw, self.nc
        nw = len(wlist)
        wv = [w.rearrange("(kc p) n -> p kc n", p=128) for w in wlist]
        tiles = cfg.tiles
        with ExitStack() as st:
            Tmax = max(sum(tiles[i][1] for i in sup) for sup in sups)
            act = self.sb(st, "lin_act", [128, KC, Tmax], BF16)
            actB = [Buf() for _ in range(max(len(s) for s in sups))]
            wst = [self.sb(st, "lin_wst", [128, KC, nw, NG], F32) for _ in range(nst)]
            wstB = [[Buf() for _ in range(nw)] for _ in range(nst)]
            wbf = [self.sb(st, "lin_wbf", [128, KC, nw, NG], BF16) for _ in range(2)]
            wbfB = [[Buf(), Buf()], [Buf(), Buf()]]
            ng = N // NG

            def load_w(g):
                bf = g % 2
                sf = g % nst
                for wi in range(nw):
                    fw.dma(wst[sf][:, :, wi, :], wv[wi][:, :, g * NG:(g + 1) * NG], writes=[wstB[sf][wi]])
                k1 = max(1, KC // 4)
                fw.op(fw.pool, lambda: nc.gpsimd.tensor_copy(out=wbf[bf][:, 0:k1].rearrange("p a b c -> p (a b c)"),
                                                              in_=wst[sf][:, 0:k1].rearrange("p a b c -> p (a b c)")),
                      reads=wstB[sf], writes=[wbfB[bf][0]])
                fw.op(fw.act, lambda: nc.scalar.copy(out=wbf[bf][:, k1:KC].rearrange("p a b c -> p (a b c)"),
                                                     in_=wst[sf][:, k1:KC].rearrange("p a b c -> p (a b c)")),
                      reads=wstB[sf], writes=[wbfB[bf][1]])

            for sup in sups:
                load_w(0)
                slots, off = [], 0
                for si, ti in enumerate(sup):
                    n = tiles[ti][1]
                    slots.append((ti, off, n, actB[si]))
                    load_act(ti, act, off, n, actB[si])
                    off += n
                if sup_begin is not None:
                    sup_begin(sup)
                for g in range(ng):
                    if g + 1 < ng:
                        load_w(g + 1)
                    bf = g % 2
                    if modeA is not None and modeA(g):
                        for (ti, off, n, aB) in slots:
                            for blk in range(n // 128):
                                b = self.next_bank()
                                for kc in range(KC):
                                    fw.op(fw.pe, lambda: nc.tensor.matmul(
                                        self.PS[b][:, 0:NG], lhsT=act[:, kc, off + blk * 128:off + (blk + 1) * 128],
                                        rhs=wbf[bf][:, kc, 0, :], start=(kc == 0), stop=(kc == KC - 1)),
                                        reads=[wbfB[bf][0], wbfB[bf][1], aB], writes=[self.PSB[b]])
                                epi(("A", g, blk), ti, [self.PS[b][:, 0:NG]], [self.PSB[b]])
                        continue
                    units = [(nn, sl) for nn in range(NG // 128) for sl in slots]
                    pres = {}
                    for ui_, (nn, (ti, off, n, aB)) in enumerate(units):
                        nchunk = g * (NG // 128) + nn
                        if pre is not None:
                            for k_ in (ui_, ui_ + 1):
                                if k_ < len(units) and k_ not in pres:
                                    pres[k_] = pre(g * (NG // 128) + units[k_][0], units[k_][1][0])
                        if True:
                            banks = [self.next_bank() for _ in range(nw)]
                            for wi in range(nw):
                                b = banks[wi]
                                for kc in range(KC):
                                    fw.op(fw.pe, lambda: nc.tensor.matmul(
                                        self.PS[b][:, 0:n], lhsT=wbf[bf][:, kc, wi, nn * 128:(nn + 1) * 128],
                                        rhs=act[:, kc, off:off + n], start=(kc == 0), stop=(kc == KC - 1)),
                                        reads=[wbfB[bf][0], wbfB[bf][1], aB], writes=[self.PSB[b]])
                            if pre is not None:
                                epi(nchunk, ti, [self.PS[b][:, 0:n] for b in banks], [self.PSB[b] for b in banks], pres.pop(ui_))
                            else:
                                epi(nchunk, ti, [self.PS[b][:, 0:n] for b in banks], [self.PSB[b] for b in banks])
            fw.barrier()

    def mk_mod_loader(self, st, g_shift, g_scale, div=2):
        cfg, fw, nc = self.cfg, self.fw, self.nc
        DC = cfg.DC
        hc = max(1, (DC + div - 1) // div)
        stage = Rot(self, st, "modst", [128, hc, 512], F32, 2)

        def load(ti, act, off, n, aB):
            t0 = cfg.tiles[ti][0]
            for c0 in range(0, DC, hc):
                cn = min(hc, DC - c0)
                sg, sB = stage.next()
                fw.dma(sg[:, 0:cn, 0:n], self.HA[:, c0:c0 + cn, t0:t0 + n], writes=[sB])
                for k in range(cn):
                    kc = c0 + k
                    fw.op(fw.act, lambda: nc.scalar.activation(
                        out=act[:, kc, off:off + n], in_=sg[:, k, 0:n], func=AF.Identity,
                        scale=self.mcol(g_scale, kc, ti), bias=self.mcol(g_shift, kc, ti)),
                        reads=[sB, self.modvB], writes=[aB])
        return load

    def mk_direct_loader(self, src, KC):
        fw = self.fw

        def load(ti, act, off, n, aB):
            t0 = self.cfg.tiles[ti][0]
            fw.dma(act[:, :, off:off + n], src[:, 0:KC, t0:t0 + n], writes=[aB])
        return load

    def mk_resid_epi(self, st, g_gate):
        cfg, fw, nc = self.cfg, self.fw, self.nc
        hin = Rot(self, st, "rs_h", [128, 512], F32, 4)
        rout = Rot(self, st, "rs_o", [128, 512], F32, 4)

        def pre(nchunk, ti):
            t0, n = cfg.tiles[ti]
            ht, hB = hin.next()
            fw.dma(ht[:, 0:n], self.HA[:, nchunk, t0:t0 + n], writes=[hB])
            return ht, hB

        def epi(nchunk, ti, ps, psB, pr_):
            t0, n = cfg.tiles[ti]
            ht, hB = pr_
            ot, oB = rout.next()
            fw.op(fw.dve, lambda: nc.vector.scalar_tensor_tensor(
                out=ot[:, 0:n], in0=ps[0], scalar=self.mcol(g_gate, nchunk, ti), in1=ht[:, 0:n],
                op0=ALU.mult, op1=ALU.add), reads=[psB[0], hB, self.modvB], writes=[oB])
            fw.dma(self.R[:, nchunk, t0:t0 + n], ot[:, 0:n], reads=[oB])
        return epi, pre

    def phase_ln(self, which):
        cfg, fw, nc = self.cfg, self.fw, self.nc
        DC, D = cfg.DC, cfg.D
        gcol, bcol = 2 * which, 2 * which + 1
        R32 = mybir.dt.float32r if LN_FP32R else F32
        with ExitStack() as st:
            rin = [(self.sb(st, "ln_r", [128, DC, 512]), [Buf() for _ in range(DC)]) for _ in range(2)]
            rout = [(self.sb(st, "ln_o", [128, DC, 512]), [Buf() for _ in range(DC)]) for _ in range(2)]
            sqt = self.sb(st, "ln_sq", [128, DC, 512])
            sqB = Buf()
            small = Rot(self, st, "ln_s", [128, 512], F32, 8)
            def ld(it):
                t0_, n_ = cfg.tiles[it]
                fw.dma(rin[it % 2][0][:, :, 0:n_], self.R[:, :, t0_:t0_ + n_], writes=rin[it % 2][1])

            ld(0)
            for it, (t0, n) in enumerate(cfg.tiles):
                rt, rBs = rin[it % 2]
                ot, oBs = rout[it % 2]
                if it + 1 < len(cfg.tiles):
                    ld(it + 1)
                fw.op(fw.act, lambda: nc.scalar.activation(out=sqt[:, :, 0:n], in_=rt[:, :, 0:n], func=AF.Square),
                      reads=rBs, writes=[sqB])
                b1, b2 = self.next_bank(), self.next_bank()
                for c in range(DC):
                    fw.op(fw.pe, lambda: nc.tensor.matmul(self.PS[b1][:, 0:n], lhsT=self.ones_f[:].bitcast(R32), rhs=rt[:, c, 0:n].bitcast(R32),
                                                          start=(c == 0), stop=(c == DC - 1)),
                          reads=[rBs[c], self.onesB], writes=[self.PSB[b1]])
                for c in range(DC):
                    fw.op(fw.pe, lambda: nc.tensor.matmul(self.PS[b2][:, 0:n], lhsT=self.ones_f[:].bitcast(R32), rhs=sqt[:, c, 0:n].bitcast(R32),
                                                          start=(c == 0), stop=(c == DC - 1)),
                          reads=[sqB, self.onesB], writes=[self.PSB[b2]])
                mean, mB = small.next()
                fw.op(fw.act, lambda: nc.scalar.mul(out=mean[:, 0:n], in_=self.PS[b1][:, 0:n], mul=1.0 / D),
                      reads=[self.PSB[b1]], writes=[mB])
                msq, qB = small.next()
                fw.op(fw.pool, lambda: nc.gpsimd.tensor_tensor(out=msq[:, 0:n], in0=mean[:, 0:n], in1=mean[:, 0:n], op=ALU.mult),
                      reads=[mB], writes=[qB])
                var, vB = small.next()
                fw.op(fw.dve, lambda: nc.vector.scalar_tensor_tensor(out=var[:, 0:n], in0=self.PS[b2][:, 0:n], scalar=1.0 / D,
                                                                      in1=msq[:, 0:n], op0=ALU.mult, op1=ALU.subtract),
                      reads=[self.PSB[b2], qB], writes=[vB])
                fw.op(fw.act, lambda: nc.scalar.activation(out=var[:, 0:n], in_=var[:, 0:n], func=AF.Sqrt, bias=LN_EPS, scale=1.0),
                      reads=[vB], writes=[vB])
                rstd, sB = small.next()
                fw.op(fw.dve, lambda: nc.vector.reciprocal(out=rstd[:, 0:n], in_=var[:, 0:n]), reads=[vB], writes=[sB])
                for c in range(DC):
                    eng = fw.dve if c % 2 == 0 else fw.pool
                    e = nc.vector if c % 2 == 0 else nc.gpsimd
                    fw.op(eng, lambda: e.tensor_tensor(out=rt[:, c, 0:n], in0=rt[:, c, 0:n], in1=mean[:, 0:n], op=ALU.subtract),
                          reads=[mB], writes=[rBs[c]])
                    fw.op(eng, lambda: e.tensor_tensor(out=rt[:, c, 0:n], in0=rt[:, c, 0:n], in1=rstd[:, 0:n], op=ALU.mult),
                          reads=[sB], writes=[rBs[c]])
                    fw.op(fw.act, lambda: nc.scalar.activation(out=ot[:, c, 0:n], in_=rt[:, c, 0:n], func=AF.Identity,
                                                               scale=self.ln[:, gcol, c:c + 1], bias=self.ln[:, bcol, c:c + 1]),
                          reads=[rBs[c], self.lnB], writes=[oBs[c]])
                fw.dma(self.HA[:, :, t0:t0 + n], ot[:, :, 0:n], reads=oBs)
            fw.barrier()

    def ffn(self, l):
        cfg, fw, nc = self.cfg, self.fw, self.nc
        with ExitStack() as st:
            loader = self.mk_mod_loader(st, 3, 4)
            sg = Rot(self, st, "ff_sg", [128, 512], F32, 3)
            ho = Rot(self, st, "ff_ho", [128, 512], BF16, 4)

            def epi(nchunk, ti, ps, psB):
                t0, n = cfg.tiles[ti]
                s, sB = sg.next()
                fw.op(fw.act, lambda: nc.scalar.activation(out=s[:, 0:n], in_=ps[0], func=AF.Silu), reads=[psB[0]], writes=[sB])
                h, hB = ho.next()
                fw.op(fw.dve, lambda: nc.vector.tensor_tensor(out=h[:, 0:n], in0=ps[1], in1=s[:, 0:n], op=ALU.mult),
                      reads=[psB[1], sB], writes=[hB])
                fw.dma(self.AFF[:, nchunk, t0:t0 + n], h[:, 0:n], reads=[hB])

            self.linear(cfg.DC, [self.din["ffn_w_gate"][l], self.din["ffn_w_up"][l]], cfg.F, 256 if cfg.F % 256 == 0 else 128, cfg.sups(2304), loader, epi, nst=1)
        with ExitStack() as st:
            epi, pre = self.mk_resid_epi(st, 5)
            self.linear(cfg.FC, [self.din["ffn_w_down"][l]], cfg.D, 128, cfg.sups(1536),
                        self.mk_direct_loader(self.AFF, cfg.FC), epi, pre=pre, nst=1)

    def rec_mixer(self, j, lst):
        cfg, fw, nc = self.cfg, self.fw, self.nc
        DC, RC, T, C, BW = cfg.DC, cfg.RC, cfg.T, cfg.C, cfg.BW
        with ExitStack() as st:
            loader = self.mk_mod_loader(st, 0, 1)
            oy = Rot(self, st, "r1_y", [128, 512], BF16, 4)
            oz = Rot(self, st, "r1_z", [128, 512], F32, 4)

            def epi(nchunk, ti, ps, psB):
                t0, n = cfg.tiles[ti]
                if nchunk < RC:
                    o, oB = oy.next()
                    fw.op(fw.act, lambda: nc.scalar.activation(out=o[:, 0:n], in_=ps[0], func=AF.Gelu), reads=[psB[0]], writes=[oB])
                    fw.dma(self.Y[:, nchunk, t0:t0 + n], o[:, 0:n], reads=[oB])
                else:
                    o, oB = oz.next()
                    fw.op(fw.dve, lambda: nc.vector.tensor_copy(out=o[:, 0:n], in_=ps[0]), reads=[psB[0]], writes=[oB])
                    fw.dma(self.ZP[:, nchunk - RC, t0:t0 + n], o[:, 0:n], reads=[oB])

            self.linear(DC, [self.din["rec_w_in"][j]], 2 * cfg.Dr, 256, cfg.sups(2304), loader, epi)
        with ExitStack() as st:
            cw = self.sb(st, "cw", [128, 4, RC]); cwB = Buf()
            for tp in range(4):
                self.load_cols(cw[:, tp, :], cwB, self.din["rec_conv_w"][j, tp, :].rearrange("(n p) -> n p", p=128), RC)
            cb = self.sb(st, "cb", [128, RC]); cbB = Buf()
            self.load_cols(cb[:, :], cbB, self.din["rec_conv_b"][j, :].rearrange("(n p) -> n p", p=128), RC)
            gb = self.sb(st, "gb", [128, 2, 2, RC]); gbB = Buf()
            cl = self.sb(st, "cl", [128, 2, RC]); clB = Buf()
            for d in range(2):
                self.load_cols(gb[:, d, 0, :], gbB, self.din["rec_gate_a_b"][j, d, :].rearrange("(n p) -> n p", p=128), RC)
                self.load_cols(gb[:, d, 1, :], gbB, self.din["rec_gate_x_b"][j, d, :].rearrange("(n p) -> n p", p=128), RC)
                self.load_cols(cl[:, d, :], clB, self.din["rec_lambda"][j, d, :].rearrange("(n p) -> n p", p=128), RC)
            fw.op(fw.act, lambda: nc.scalar.activation(out=cl[:], in_=cl[:], func=AF.Exp, scale=-1.0), reads=[clB], writes=[clB])
            fw.op(fw.act, lambda: nc.scalar.activation(out=cl[:], in_=cl[:], func=AF.Ln, bias=1.0, scale=1.0), reads=[clB], writes=[clB])
            fw.op(fw.act, lambda: nc.scalar.mul(out=cl[:], in_=cl[:], mul=-RG_C), reads=[clB], writes=[clB])
            fw.barrier()
            with ExitStack() as s2:
                zpr = Rot(self, s2, "zp", [128, T], F32, 3)
                zr = Rot(self, s2, "z", [128, T], F32, 2)
                zbr = Rot(self, s2, "zb", [128, T], BF16, 2)
                ppr = Rot(self, s2, "zpp", [128, T], F32, 2)
                zps = {}

                def ldz(c_):
                    zps[c_] = zpr.next()
                    fw.dma(zps[c_][0][:], self.ZP[:, c_, :], writes=[zps[c_][1]])

                ldz(0)
                for c in range(RC):
                    if c + 1 < RC:
                        ldz(c + 1)
                    zp, pB = zps.pop(c)
                    z, zB = zr.next()
                    for (s0, s1) in ((0, C), (C, T)):
                        fw.op(fw.act, lambda: nc.scalar.activation(out=z[:, s0:s1], in_=zp[:, s0:s1], func=AF.Identity,
                                                                   scale=cw[:, 2, c:c + 1], bias=cb[:, c:c + 1]),
                              reads=[pB, cwB, cbB], writes=[zB])
                        for (tp, do, so, ln) in ((0, 2, 0, s1 - s0 - 2), (1, 1, 0, s1 - s0 - 1), (3, 0, 1, s1 - s0 - 1)):
                            if ln < 1024:
                                fw.op(fw.dve, lambda: nc.vector.scalar_tensor_tensor(
                                    out=z[:, s0 + do:s0 + do + ln], in0=zp[:, s0 + so:s0 + so + ln], scalar=cw[:, tp, c:c + 1],
                                    in1=z[:, s0 + do:s0 + do + ln], op0=ALU.mult, op1=ALU.add),
                                    reads=[pB, cwB], writes=[zB])
                            else:
                                pp, ppB = ppr.next()
                                fw.op(fw.act, lambda: nc.scalar.activation(out=pp[:, 0:ln], in_=zp[:, s0 + so:s0 + so + ln], func=AF.Copy,
                                                                           scale=cw[:, tp, c:c + 1]),
                                      reads=[pB, cwB], writes=[ppB])
                                fw.op(fw.dve, lambda: nc.vector.tensor_tensor(out=z[:, s0 + do:s0 + do + ln], in0=z[:, s0 + do:s0 + do + ln],
                                                                              in1=pp[:, 0:ln], op=ALU.add),
                                      reads=[ppB], writes=[zB])
                    zb, bB = zbr.next()
                    fw.op(fw.pool, lambda: nc.gpsimd.tensor_copy(out=zb[:], in_=z[:]), reads=[zB], writes=[bB])
                    fw.dma(self.Z[:, c, :], z[:], reads=[zB])
                    fw.dma(self.ZB[:, c, :], zb[:], reads=[bB])
                fw.barrier()
            with ExitStack() as s2:
                zt = self.sb(s2, "g_z", [128, T]); zB = Buf()
                A = [self.sb(s2, "g_a", [128, T]) for _ in range(2)]; AB = [Buf(), Buf()]
                Bt = [self.sb(s2, "g_b", [128, T]) for _ in range(2)]; BB = [Buf(), Buf()]
                tmp = self.sb(s2, "g_t", [128, T]); tB = Buf()
                NKMAX = 4
                zbk = [self.sb(s2, "g_zb", [128, T], BF16) for _ in range(NKMAX)]; zbkB = [Buf() for _ in range(NKMAX)]
                ytr = Rot(self, s2, "g_y", [128, T], BF16, 2)
                mt = self.sb(s2, "g_m", [128, T], BF16); mB = Buf()
                wgs = Rot(self, s2, "g_ws", [128, 4 * NKMAX, 128], F32, 2)
                wgb = Rot(self, s2, "g_wb", [128, 4 * NKMAX, 128], BF16, 2)
                gws = (self.din["rec_gate_a_w"], self.din["rec_gate_x_w"])

                def prefetch(c):
                    h0, h1 = (128 * c) // BW, (128 * c + 127) // BW
                    k0, k1 = (h0 * BW) // 128, ((h1 + 1) * BW - 1) // 128
                    ks = list(range(k0, k1 + 1))
                    assert len(ks) <= NKMAX
                    ws, wsB = wgs.next()
                    wb, wbB = wgb.next()
                    fw.op(fw.pool, lambda: nc.gpsimd.memset(ws[:], 0.0), writes=[wsB])
                    for d in range(2):
                        for gt in range(2):
                            for ki, k in enumerate(ks):
                                tix = (d * 2 + gt) * NKMAX + ki
                                for h in range(h0, h1 + 1):
                                    rl, rh = max(h * BW, 128 * k), min((h + 1) * BW, 128 * k + 128)
                                    c_l, c_h = max(h * BW, 128 * c), min((h + 1) * BW, 128 * c + 128)
                                    if rl >= rh or c_l >= c_h:
                                        continue
                                    fw.dma(ws[rl - 128 * k:rh - 128 * k, tix, c_l - 128 * c:c_h - 128 * c],
                                           gws[gt][j, d, h, rl - h * BW:rh - h * BW, c_l - h * BW:c_h - h * BW],
                                           writes=[Buf()], reads=[wsB])
                    fw.op(fw.pool, lambda: nc.gpsimd.tensor_copy(out=wb[:], in_=ws[:]), reads=[], writes=[wsB, wbB])
                    for ki, k in enumerate(ks):
                        fw.dma(zbk[ki][:], self.ZB[:, k, :], writes=[zbkB[ki]])
                    yt, yB = ytr.next()
                    fw.dma(yt[:], self.Y[:, c, :], writes=[yB])
                    return ks, wb, wbB, yt, yB

                nxt = prefetch(0)
                for c in range(RC):
                    ks, wb, wbB, yt, yB = nxt
                    fw.dma(zt[:], self.Z[:, c, :], writes=[zB])
                    for tix_, (t0, n) in enumerate(cfg.tiles):
                        for d in range(2):
                            banks = []
                            for gt in range(2):
                                b = self.next_bank()
                                banks.append(b)
                                for ki, k in enumerate(ks):
                                    tix = (d * 2 + gt) * NKMAX + ki
                                    fw.op(fw.pe, lambda: nc.tensor.matmul(self.PS[b][:, 0:n], lhsT=wb[:, tix, :], rhs=zbk[ki][:, t0:t0 + n],
                                                                          start=(ki == 0), stop=(ki == len(ks) - 1)),
                                          reads=[wbB, zbkB[ki]], writes=[self.PSB[b]])
                            fw.op(fw.act, lambda: nc.scalar.activation(out=A[d][:, t0:t0 + n], in_=self.PS[banks[0]][:, 0:n], func=AF.Sigmoid,
                                                                       bias=gb[:, d, 0, c:c + 1], scale=1.0),
                                  reads=[self.PSB[banks[0]], gbB], writes=[AB[d]])
                            fw.op(fw.act, lambda: nc.scalar.activation(out=Bt[d][:, t0:t0 + n], in_=self.PS[banks[1]][:, 0:n], func=AF.Sigmoid,
                                                                       bias=gb[:, d, 1, c:c + 1], scale=1.0),
                                  reads=[self.PSB[banks[1]], gbB], writes=[BB[d]])
                    if c + 1 < RC:
                        nxt = prefetch(c + 1)
                    fw.op(fw.dve, lambda: nc.vector.tensor_tensor(out=Bt[0][:], in0=Bt[0][:], in1=zt[:], op=ALU.mult),
                          reads=[zB], writes=[BB[0]])
                    fw.op(fw.pool, lambda: nc.gpsimd.tensor_tensor(out=Bt[1][:], in0=Bt[1][:], in1=zt[:], op=ALU.mult),
                          reads=[zB], writes=[BB[1]])
                    for d in range(2):
                        fw.op(fw.act, lambda: nc.scalar.activation(out=A[d][:], in_=A[d][:], func=AF.Exp, scale=cl[:, d, c:c + 1]),
                              reads=[clB], writes=[AB[d]])
                    for d, (sc_, sB_) in enumerate(((tmp, tB), (zt, zB))):
                        fw.op(fw.act, lambda: nc.scalar.activation(out=sc_[:], in_=A[d][:], func=AF.Square), reads=[AB[d]], writes=[sB_])
                        fw.op(fw.act, lambda: nc.scalar.activation(out=sc_[:], in_=sc_[:], func=AF.Sqrt, scale=-1.0, bias=1.0),
                              reads=[sB_], writes=[sB_])
                    fw.op(fw.dve, lambda: nc.vector.tensor_tensor(out=Bt[0][:], in0=Bt[0][:], in1=tmp[:], op=ALU.mult),
                          reads=[tB], writes=[BB[0]])
                    fw.op(fw.dve, lambda: nc.vector.tensor_tensor_scan(out=tmp[:, :], data0=A[0][:, :], data1=Bt[0][:, :], initial=0.0,
                                                                       op0=ALU.mult, op1=ALU.add),
                          reads=[AB[0], BB[0]], writes=[tB])
                    fw.op(fw.dve, lambda: nc.vector.tensor_tensor(out=Bt[1][:], in0=Bt[1][:], in1=zt[:], op=ALU.mult),
                          reads=[zB], writes=[BB[1]])
                    fw.op(fw.dve, lambda: nc.vector.tensor_tensor_scan(out=zt[:, 0:C][:, ::-1], data0=A[1][:, 0:C][:, ::-1],
                                                                       data1=Bt[1][:, 0:C][:, ::-1], initial=0.0,
                                                                       op0=ALU.mult, op1=ALU.add),
                          reads=[AB[1], BB[1]], writes=[zB])
                    fw.op(fw.dve, lambda: nc.vector.tensor_tensor_scan(out=zt[:, C:T][:, ::-1], data0=A[1][:, C:T][:, ::-1],
                                                                       data1=Bt[1][:, C:T][:, ::-1], initial=zt[:, 0:1],
                                                                       op0=ALU.mult, op1=ALU.add),
                          reads=[AB[1], BB[1]], writes=[zB])
                    fw.op(fw.dve, lambda: nc.vector.tensor_tensor(out=tmp[:], in0=tmp[:], in1=zt[:], op=ALU.add), reads=[zB], writes=[tB])
                    fw.op(fw.pool, lambda: nc.gpsimd.tensor_tensor(out=mt[:], in0=tmp[:], in1=yt[:], op=ALU.mult),
                          reads=[tB, yB], writes=[mB])
                    fw.dma(self.M[:, c, :], mt[:], reads=[mB])
                fw.barrier()
        with ExitStack() as st:
            epi, pre = self.mk_resid_epi(st, 2)
            self.linear(RC, [self.din["rec_w_out"][j]], cfg.D, 256, cfg.sups(2304), self.mk_direct_loader(self.M, RC), epi, pre=pre, nst=1)

    def att_mixer(self, j, lst, last):
        cfg, fw, nc = self.cfg, self.fw, self.nc
        DC, T, C, S, H, KV, G = cfg.DC, cfg.T, cfg.C, cfg.S, cfg.H, cfg.KV, cfg.G
        GW = G * 128
        with ExitStack() as ast:
            with ExitStack() as st:
                loader = self.mk_mod_loader(st, 0, 1, div=2)
                sups = cfg.sups(2304)
                vo = Rot(self, st, "vo", [128, 256], BF16, 4)
                Tmax = max(sum(cfg.tiles[i][1] for i in sup) for sup in sups)
                cs = self.sb(st, "cos", [128, Tmax]); sn = self.sb(st, "sin", [128, Tmax]); csB = Buf()
                state = {}

                def sup_begin(sup):
                    off = 0
                    state["off"] = {}
                    for ti in sup:
                        t0, n = cfg.tiles[ti]
                        state["off"][ti] = off
                        if ti != 0:
                            fw.dma(cs[:, off:off + n], self.din["cos2"][:, t0 - C:t0 - C + n], writes=[csB])
                            fw.dma(sn[:, off:off + n], self.din["sinx"][:, t0 - C:t0 - C + n], writes=[csB])
                        off += n

                t1r = Rot(self, st, "rp1", [128, 512], F32, 3)
                t2r = Rot(self, st, "rp2", [128, 512], F32, 3)
                qo = Rot(self, st, "qo", [128, 512], BF16, 4)
                NGQ = 256
                gv0 = (H + KV) * 128 // NGQ

                def epi(nchunk, ti, ps, psB):
                    t0, n = cfg.tiles[ti]
                    if isinstance(nchunk, tuple):
                        _, g, blk = nchunk
                        col0 = (g - gv0) * NGQ
                        vt, vB = vo.next()
                        fw.op(fw.act, lambda: nc.scalar.copy(out=vt[:, 0:NGQ], in_=ps[0]), reads=[psB[0]], writes=[vB])
                        fw.dma(self.VD[:, t0 // 128 + blk, col0:col0 + NGQ], vt[:, 0:NGQ], reads=[vB])
                        return
                    isq = nchunk < H
                    dst, dB = qo.next()
                    dst = dst[:, 0:n]
                    if ti == 0:
                        fw.op(fw.act, lambda: nc.scalar.copy(out=dst, in_=ps[0]), reads=[psB[0]], writes=[dB])
                    else:
                        o = state["off"][ti]
                        t1, t1B = t1r.next()
                        t2, t2B = t2r.next()
                        fw.op(fw.dve, lambda: nc.vector.tensor_tensor(out=t1[:, 0:n], in0=ps[0], in1=cs[:, o:o + n], op=ALU.mult),
                              reads=[psB[0], csB], writes=[t1B])
                        fw.op(fw.dve, lambda: nc.vector.tensor_tensor(out=t2[0:64, 0:n], in0=ps[0][64:128, :], in1=sn[64:128, o:o + n], op=ALU.mult),
                              reads=[psB[0], csB], writes=[t2B])
                        fw.op(fw.dve, lambda: nc.vector.tensor_tensor(out=t2[64:128, 0:n], in0=ps[0][0:64, :], in1=sn[0:64, o:o + n], op=ALU.mult),
                              reads=[psB[0], csB], writes=[t2B])
                        fw.op(fw.pool, lambda: nc.gpsimd.tensor_tensor(out=dst, in0=t1[:, 0:n], in1=t2[:, 0:n], op=ALU.add),
                              reads=[t1B, t2B], writes=[dB])
                    if isq:
                        fw.dma(self.Q[:, nchunk, t0:t0 + n], dst, reads=[dB])
                    else:
                        fw.dma(self.KD[:, nchunk - H, t0:t0 + n], dst, reads=[dB])

                self.linear(DC, [self.din["att_w_qkv"][j]], cfg.NQKV, NGQ, sups, loader, epi,
                            modeA=lambda g: g >= gv0, sup_begin=sup_begin)
            with ExitStack() as st:
                kT = self.sb(st, "kT", [128, KV, T], BF16)
                kTB = Buf()
                V = self.sb(st, "V", [128, T // 128, KV * 128], BF16)
                VB = Buf()
                fw.dma(kT[:], self.KD[:, :, :], writes=[kTB])
                fw.dma(V[:], self.VD[:, :, :], writes=[VB])
                es = self.sb(st, "esink", [128, H]); esB = Buf()
                fw.dma(es[:], self.din["att_sink"][j:j + 1, :].to_broadcast([128, H]), writes=[esB])
                fw.op(fw.act, lambda: nc.scalar.activation(out=es[:], in_=es[:], func=AF.Exp), reads=[esB], writes=[esB])
                esf = self.sb(st, "esf", [128, H, 128]); esfB = Buf()
                fw.op(fw.pool, lambda: nc.gpsimd.memset(esf[:], 0.0), writes=[esfB])
                for h in range(H):
                    fw.op(fw.pool, lambda: nc.gpsimd.tensor_scalar(out=esf[:, h, :], in0=esf[:, h, :], scalar1=es[:, h:h + 1], scalar2=None,
                                                                   op0=ALU.add), reads=[esB], writes=[esfB])
                mk32 = self.sb(st, "mk32", [128, 2, 128]); mk = self.sb(st, "mk", [128, 2, G, 128], BF16); mkB = Buf()
                fw.dma(mk32[:, 0, :], self.din["mask_prev"][:, :], writes=[mkB])
                fw.dma(mk32[:, 1, :], self.din["mask_next"][:, :], writes=[mkB])
                fw.op(fw.pool, lambda: nc.gpsimd.tensor_scalar(out=mk32[:], in0=mk32[:], scalar1=-1.0, scalar2=30000.0,
                                                               op0=ALU.add, op1=ALU.mult), reads=[mkB], writes=[mkB])
                for m in range(2):
                    for g in range(G):
                        fw.op(fw.pool, lambda: nc.gpsimd.tensor_copy(out=mk[:, m, g, :], in_=mk32[:, m, :]), reads=[mkB], writes=[mkB])
                idb = self.sb(st, "identb", [128, 128], BF16); idbB = Buf()
                fw.op(fw.pool, lambda: nc.gpsimd.tensor_copy(out=idb[:], in_=self.ident[:]), reads=[self.identB], writes=[idbB])
                qr = Rot(self, st, "aq", [128, H, 128], BF16, 4)
                pr = Rot(self, st, "ap", [128, GW], BF16, 5)
                orr = Rot(self, st, "ao", [128, H, 128], BF16, 2)
                dr = Rot(self, st, "ad", [128, GW], F32, 2)
                NB = S // 128
                blocks = []
                if True:
                    for qb in range(C // 128):
                        blocks.append((qb * 128, [(kb * 128, None) for kb in range(C // 128)]))
                for jb in range(NB):
                    ch = []
                    for o_, mt_ in ((-1, 0), (0, None), (1, 1)):
                        if 0 <= jb + o_ < NB:
                            ch.append((C + (jb + o_) * 128, mt_))
                    ch += [(kb * 128, None) for kb in range(C // 128)]
                    blocks.append((C + jb * 128, ch))
                units = []
                for bi, (qoff, ch) in enumerate(blocks):
                    for kh in range(KV):
                        for ci, (koff, mt_) in enumerate(ch):
                            units.append((bi, kh, ci, koff, mt_, len(ch)))
                qtiles = {}

                def ensure_q(bi):
                    if bi < len(blocks) and bi not in qtiles:
                        qt, qB = qr.next()
                        qoff = blocks[bi][0]
                        fw.dma(qt[:], self.Q[:, :, qoff:qoff + 128], writes=[qB])
                        qtiles[bi] = (qt, qB)

                sc = cfg.HD ** -0.5
                SBANK = [0, 1, 2, 3]
                pts = {}

                def emitS(ui):
                    bi, kh, ci, koff, mt_, nch = units[ui]
                    ensure_q(bi)
                    ensure_q(bi + 1)
                    qt, qB = qtiles[bi]
                    b = SBANK[ui % 4]
                    fw.op(fw.pe, lambda: nc.tensor.matmul(self.PS[b][:, 0:GW], lhsT=kT[:, kh, koff:koff + 128],
                                                          rhs=qt[:, kh * G:(kh + 1) * G, :].rearrange("p a b -> p (a b)"),
                                                          start=True, stop=(mt_ is None)),
                          reads=[kTB, qB], writes=[self.PSB[b]])
                    if mt_ is not None:
                        fw.op(fw.pe, lambda: nc.tensor.matmul(self.PS[b][:, 0:GW], lhsT=idb[:], rhs=mk[:, mt_, :, :].rearrange("p a b -> p (a b)"),
                                                              start=False, stop=True),
                              reads=[mkB, idbB], writes=[self.PSB[b]])

                cur = {}

                def emitPV(ui):
                    bi, kh, ci, koff, mt_, nch = units[ui]
                    b = SBANK[ui % 4]
                    pt, pB = pr.next()
                    fw.op(fw.act, lambda: nc.scalar.activation(out=pt[:], in_=self.PS[b][:, 0:GW], func=AF.Exp, scale=sc),
                          reads=[self.PSB[b]], writes=[pB])
                    pair = bi * KV + kh
                    ob, db = 4 + pair % 2, 6 + pair % 2
                    fw.op(fw.pe, lambda: nc.tensor.matmul(self.PS[ob][:, 0:GW], lhsT=V[:, koff // 128, kh * 128:(kh + 1) * 128], rhs=pt[:],
                                                          start=(ci == 0), stop=(ci == nch - 1)),
                          reads=[VB, pB], writes=[self.PSB[ob]])
                    fw.op(fw.pe, lambda: nc.tensor.matmul(self.PS[db][:, 0:GW], lhsT=self.ones_b[:], rhs=pt[:],
                                                          start=(ci == 0), stop=(ci == nch - 1)),
                          reads=[self.onesB, pB], writes=[self.PSB[db]])
                    if ci == nch - 1:
                        if kh == 0:
                            cur["o"] = orr.next()
                        ot, oB = cur["o"]
                        dt_, dB = dr.next()
                        fw.op(fw.dve, lambda: nc.vector.tensor_tensor(out=dt_[:], in0=self.PS[db][:, 0:GW],
                                                                      in1=esf[:, kh * G:(kh + 1) * G, :].rearrange("p a b -> p (a b)"), op=ALU.add),
                              reads=[self.PSB[db], esfB], writes=[dB])
                        fw.op(fw.dve, lambda: nc.vector.reciprocal(out=dt_[:], in_=dt_[:]), reads=[dB], writes=[dB])
                        fw.op(fw.dve, lambda: nc.vector.tensor_tensor(out=ot[:, kh * G:(kh + 1) * G, :].rearrange("p a b -> p (a b)"),
                                                                      in0=self.PS[ob][:, 0:GW], in1=dt_[:], op=ALU.mult),
                              reads=[self.PSB[ob], dB], writes=[oB])
                        if kh == KV - 1:
                            qoff = blocks[bi][0]
                            fw.dma(self.O[:, :, qoff:qoff + 128], ot[:], reads=[oB])
                            qtiles.pop(bi, None)

                emitS(0)
                if len(units) > 1:
                    emitS(1)
                for ui in range(len(units)):
                    if ui + 2 < len(units):
                        emitS(ui + 2)
                    emitPV(ui)
                fw.barrier()
        with ExitStack() as st:
            epi, pre = self.mk_resid_epi(st, 2)
            self.linear(DC, [self.din["att_w_o"][j]], cfg.D, 256, cfg.sups(2304), self.mk_direct_loader(self.O, DC), epi, pre=pre)


def input_shapes(cfg):
    D, S, C, Dr, F, L = cfg.D, cfg.S, cfg.C, cfg.Dr, cfg.F, cfg.depth
    return {
        "x": (S, D), "ctx": (C, D), "c": (1, D), "c_ctx": (1, D),
        "mod_w": (L, D, 6 * D), "mod_b": (L, 6 * D),
        "ln_mix_g": (L, D), "ln_mix_b": (L, D), "ln_ffn_g": (L, D), "ln_ffn_b": (L, D),
        "ffn_w_gate": (L, D, F), "ffn_w_up": (L, D, F), "ffn_w_down": (L, F, D),
        "rec_w_in": (cfg.n_rec, D, 2 * Dr), "rec_conv_w": (cfg.n_rec, 4, Dr), "rec_conv_b": (cfg.n_rec, Dr),
        "rec_gate_a_w": (cfg.n_rec, 2, cfg.NBLK, cfg.BW, cfg.BW), "rec_gate_a_b": (cfg.n_rec, 2, Dr),
        "rec_gate_x_w": (cfg.n_rec, 2, cfg.NBLK, cfg.BW, cfg.BW), "rec_gate_x_b": (cfg.n_rec, 2, Dr),
        "rec_lambda": (cfg.n_rec, 2, Dr), "rec_w_out": (cfg.n_rec, Dr, D),
        "att_w_qkv": (max(cfg.n_att, 1), D, cfg.NQKV), "att_sink": (max(cfg.n_att, 1), cfg.H),
        "att_w_o": (max(cfg.n_att, 1), D, D),
        "ident": (128, 128), "cos2": (128, S), "sinx": (128, S), "mask_prev": (128, 128), "mask_next": (128, 128),
    }


def constants(cfg):
    S = cfg.S
    t = np.arange(S)
    row = (t // cfg.grid_w).astype(np.float32)
    col = (t % cfg.grid_w).astype(np.float32)
    axis_dim = cfg.HD // 2
    inv = (cfg.rope_base ** (-np.arange(0, axis_dim, 2, dtype=np.float32) / axis_dim)).astype(np.float32)
    ang = np.concatenate([row[:, None] * inv, col[:, None] * inv], axis=-1).astype(np.float32)
    cos, sin = np.cos(ang).T.astype(np.float32), np.sin(ang).T.astype(np.float32)
    p = np.arange(128)[:, None]
    i = np.arange(128)[None, :]
    return {
        "ident": np.eye(128, dtype=np.float32),
        "cos2": np.ascontiguousarray(np.concatenate([cos, cos], 0)),
        "sinx": np.ascontiguousarray(np.concatenate([sin, -sin], 0)),
        "mask_prev": (p >= i).astype(np.float32),
        "mask_next": (p <= i).astype(np.float32),
    }


_CACHE = {}


def run(cfg, inputs, n_cores):
    key = (cfg.D, cfg.S, cfg.depth)
    if key not in _CACHE:
        _CACHE[key] = KB(cfg).build()
    nc = _CACHE[key]
    consts = constants(cfg)
    f = lambda a: np.ascontiguousarray(np.asarray(a, dtype=np.float32))
    shared = {k: f(v) for k, v in inputs.items() if k not in ("x", "c", "ctx", "c_ctx")}
    shared.update(consts)
    shared["c_ctx"] = f(inputs["c_ctx"]).reshape(1, -1)
    in_maps = []
    for b in range(n_cores):
        m = dict(shared)
        m["x"] = f(inputs["x"][b])
        m["ctx"] = f(inputs["ctx"][b])
        m["c"] = f(inputs["c"][b:b + 1])
        in_maps.append(m)
    res = run_bass_kernel_spmd(nc, in_maps, core_ids=list(range(n_cores)))
    return np.stack([np.asarray(r["out"], dtype=np.float32) for r in res.results], 0)


def kernel(**inputs):
    cfg = Cfg()
    B = inputs["x"].shape[0]
    return run(cfg, inputs, B)
```

```python
import math
from contextlib import ExitStack

import numpy as np
import concourse.bass as bass
import concourse.mybir as mybir
from concourse.bass_utils import run_bass_kernel_spmd

F32 = mybir.dt.float32
BF16 = mybir.dt.bfloat16
ALU = mybir.AluOpType
AF = mybir.ActivationFunctionType
LN_EPS = 1e-5
LN_FP32R = False
STORES_ON_ACT = False
RG_C = 8.0


class Cfg:
    def __init__(self, D=2048, S=4096, C=256, Dr=2816, NBLK=16, F=5632, HD=128, KV=4,
                 depth=4, grid_w=64, rope_base=10000.0):
        self.D, self.S, self.C, self.Dr, self.NBLK, self.F = D, S, C, Dr, NBLK, F
        self.HD, self.KV, self.depth = HD, KV, depth
        self.H = D // HD
        self.G = self.H // KV
        self.BW = Dr // NBLK
        self.DC, self.RC, self.FC = D // 128, Dr // 128, F // 128
        self.T = C + S
        self.NQKV = (self.H + 2 * KV) * HD
        self.grid_w, self.rope_base = grid_w, rope_base
        self.alpha = (2.0 * depth) ** 0.25
        self.n_rec = (depth + 1) // 2
        self.n_att = depth // 2
        self.tiles = [(0, C)] + [(C + 512 * j, 512) for j in range(S // 512)]

    def sups(self, maxtok):
        out, cur, n = [], [], 0
        for i, (_, t) in enumerate(self.tiles):
            if cur and n + t > maxtok:
                out.append(cur)
                cur, n = [], 0
            cur.append(i)
            n += t
        if cur:
            out.append(cur)
        return out


class Buf:
    __slots__ = ("w", "r")

    def __init__(self):
        self.w = {}
        self.r = {}


class Eng:
    def __init__(self, e, sem, sid, name):
        self.e, self.sem, self.sid, self.name = e, sem, sid, name
        self.count = 0
        self.waited = {}


class FW:
    def __init__(self, nc, stack, n_dma_sems=24):
        self.nc = nc
        self.sems = []

        def mk(name):
            s = stack.enter_context(nc.semaphore(name))
            self.sems.append(s)
            return s, len(self.sems) - 1

        self.pe = Eng(nc.tensor, *mk("c_pe"), "pe")
        self.act = Eng(nc.scalar, *mk("c_act"), "act")
        self.dve = Eng(nc.vector, *mk("c_dve"), "dve")
        self.pool = Eng(nc.gpsimd, *mk("c_pool"), "pool")
        self.sp = Eng(nc.sync, None, None, "sp")
        self.engs = [self.pe, self.act, self.dve, self.pool, self.sp]
        self.dsems = []
        for i in range(n_dma_sems):
            s, sid = mk("d%d" % i)
            self.dsems.append([s, sid, 0])
        self.dnext = 0
        self.bar_sem, self.bar_sid = mk("bar")
        self.bar_cnt = 0
        self.bar_src = self.bar_dst = None

    def _wait(self, eng, sid, val):
        if val <= 0 or eng.waited.get(sid, 0) >= val:
            return
        eng.e.wait_ge(self.sems[sid], val)
        eng.waited[sid] = val

    def _deps(self, eng, reads, writes):
        deps = {}
        for b in reads:
            for s, v in b.w.items():
                if deps.get(s, 0) < v:
                    deps[s] = v
        for b in writes:
            for s, v in b.w.items():
                if deps.get(s, 0) < v:
                    deps[s] = v
            for s, v in b.r.items():
                if deps.get(s, 0) < v:
                    deps[s] = v
        for s, v in deps.items():
            if eng is self.pe and s == self.pe.sid:
                continue
            self._wait(eng, s, v)

    def _mark(self, sid, val, reads, writes):
        for b in reads:
            if b.r.get(sid, 0) < val:
                b.r[sid] = val
        for b in writes:
            b.w = {sid: val}
            b.r = {}

    def op(self, eng, fn, reads=(), writes=()):
        self._deps(eng, reads, writes)
        ins = fn()
        eng.count += 1
        ins.then_inc(eng.sem, 1)
        self._mark(eng.sid, eng.count, reads, writes)
        return ins

    def dma(self, out, in_, reads=(), writes=(), **kw):
        q = self.act if (STORES_ON_ACT and str(out.space).endswith("DRAM")) else self.sp
        d = self.dsems[self.dnext]
        self.dnext = (self.dnext + 1) % len(self.dsems)
        self._wait(q, d[1], d[2])
        self._deps(q, reads, writes)
        ins = q.e.dma_start(out=out, in_=in_, **kw)
        d[2] += 16
        ins.then_inc(d[0], 16)
        self._mark(d[1], d[2], reads, writes)
        return ins

    def barrier(self):
        sp = self.sp
        for e in (self.pe, self.act, self.dve, self.pool):
            self._wait(sp, e.sid, e.count)
        for d in self.dsems:
            self._wait(sp, d[1], d[2])
        ins = sp.e.dma_start(out=self.bar_dst, in_=self.bar_src)
        self.bar_cnt += 16
        ins.then_inc(self.bar_sem, 16)
        for e in self.engs:
            self._wait(e, self.bar_sid, self.bar_cnt)


class Rot:
    def __init__(self, kb, st, name, shape, dt, count):
        self.items = [(st.enter_context(kb.nc.sbuf_tensor("%s_%d_%d" % (name, kb.uid(), i), shape, dt)), Buf())
                      for i in range(count)]
        self.i = 0

    def next(self):
        it = self.items[self.i]
        self.i = (self.i + 1) % len(self.items)
        return it


class KB:
    def __init__(self, cfg):
        self.cfg = cfg
        self._uid = 0
        self.nc = bass.Bass("TRN2", target_bir_lowering=False)

    def uid(self):
        self._uid += 1
        return self._uid

    def sb(self, st, name, shape, dt=F32):
        return st.enter_context(self.nc.sbuf_tensor("%s_%d" % (name, self.uid()), shape, dt))

    def build(self):
        cfg, nc = self.cfg, self.nc
        D, S, C, T, Dr, F, L = cfg.D, cfg.S, cfg.C, cfg.T, cfg.Dr, cfg.F, cfg.depth
        self.din = {}

        def I(name, shape):
            self.din[name] = nc.dram_tensor(name, list(shape), F32, kind="ExternalInput").ap()

        for name, shape in input_shapes(cfg).items():
            I(name, shape)
        self.out = nc.dram_tensor("out", [S, D], F32, kind="ExternalOutput").ap()

        def scratch(name, rows, dt):
            return nc.dram_tensor(name, [rows, T], dt).ap().rearrange("(c p) t -> p c t", p=128)

        self.HA = scratch("s_ha", D, F32)
        self.R = scratch("s_r", D, F32)
        self.Y = scratch("s_y", Dr, BF16)
        self.ZP = scratch("s_zp", Dr, F32)
        self.Z = scratch("s_z", Dr, F32)
        self.ZB = scratch("s_zb", Dr, BF16)
        self.M = scratch("s_m", Dr, BF16)
        self.Q = scratch("s_q", D, BF16)
        self.O = scratch("s_o", D, BF16)
        self.AFF = scratch("s_af", F, BF16)
        self.KD = scratch("s_k", cfg.KV * 128, BF16)
        self.VD = nc.dram_tensor("s_v", [128, (T // 128) * cfg.KV * 128], BF16).ap().rearrange("p (b n) -> p b n", n=cfg.KV * 128)

        with ExitStack() as st:
            self.fw = fw = FW(nc, st)
            b0 = self.sb(st, "bar0", [1, 8])
            b1 = self.sb(st, "bar1", [1, 8])
            fw.bar_src, fw.bar_dst = b0[:], b1[:]
            self.PS = [st.enter_context(nc.psum_tensor("ps%d" % i, [128, 512], F32)) for i in range(8)]
            self.PSB = [Buf() for _ in range(8)]
            self.bank_i = 0
            self.ident = self.sb(st, "ident", [128, 128])
            self.identB = Buf()
            fw.dma(self.ident[:], self.din["ident"][:, :], writes=[self.identB])
            self.ones_f = self.sb(st, "ones_f", [128, 128])
            self.ones_b = self.sb(st, "ones_b", [128, 128], BF16)
            self.onesB = Buf()
            fw.op(fw.pool, lambda: nc.gpsimd.memset(self.ones_f[:], 1.0), writes=[self.onesB])
            fw.op(fw.pool, lambda: nc.gpsimd.memset(self.ones_b[:], 1.0), writes=[self.onesB])
            self.colst = self.sb(st, "colst", [128, 128])
            self.colstB = Buf()
            self.silc = self.sb(st, "silc", [128, cfg.DC, 2])
            self.silcB = Buf()
            tmpc = self.sb(st, "tmpc", [128, cfg.DC])
            tmpcB = Buf()
            for j, nm in enumerate(("c", "c_ctx")):
                self.load_cols(tmpc[:, :], tmpcB, self.din[nm][0, :].rearrange("(n p) -> n p", p=128), cfg.DC)
                fw.op(fw.act, lambda: nc.scalar.activation(out=self.silc[:, :, j], in_=tmpc[:, :], func=AF.Silu),
                      reads=[tmpcB], writes=[self.silcB])
            fw.barrier()
            self.phase_in()
            for l in range(L):
                with ExitStack() as lst:
                    self.layer(l, lst)
            self.phase_out()
            fw.barrier()
        return nc

    def next_bank(self):
        b = self.bank_i
        self.bank_i = (self.bank_i + 1) % 8
        return b

    def load_cols(self, dst, dstB, src2d, n, scale=None):
        fw, nc = self.fw, self.nc
        fw.dma(self.colst[0:n, :], src2d, writes=[self.colstB])
        b = self.next_bank()
        fw.op(fw.pe, lambda: nc.tensor.transpose(self.PS[b][:, 0:n], self.colst[0:n, :], self.ident[0:n, 0:n]),
              reads=[self.colstB, self.identB], writes=[self.PSB[b]])
        if scale is None:
            fw.op(fw.act, lambda: nc.scalar.copy(out=dst, in_=self.PS[b][:, 0:n]), reads=[self.PSB[b]], writes=[dstB])
        else:
            fw.op(fw.act, lambda: nc.scalar.mul(out=dst, in_=self.PS[b][:, 0:n], mul=scale),
                  reads=[self.PSB[b]], writes=[dstB])

    def phase_in(self):
        cfg, fw, nc = self.cfg, self.fw, self.nc
        DC = cfg.DC
        with ExitStack() as st:
            xin = Rot(self, st, "xin", [128, cfg.D], F32, 3)
            stg = Rot(self, st, "xstg", [128, DC, 128], F32, 2)
            blocks = [("ctx", r, r) for r in range(0, cfg.C, 128)] + [("x", r, cfg.C + r) for r in range(0, cfg.S, 128)]
            xs_ = {}

            def ldx(i_):
                nm_, r0_, _ = blocks[i_]
                xs_[i_] = xin.next()
                fw.dma(xs_[i_][0][:], self.din[nm_][r0_:r0_ + 128, :], writes=[xs_[i_][1]])

            ldx(0)
            for bi_, (nm, r0, t0) in enumerate(blocks):
                if bi_ + 1 < len(blocks):
                    ldx(bi_ + 1)
                xt, xB = xs_.pop(bi_)
                sg, sB = stg.next()
                for g0 in range(0, DC, 4):
                    gn = min(4, DC - g0)
                    b = self.next_bank()
                    for i in range(gn):
                        fw.op(fw.pe, lambda: nc.tensor.transpose(self.PS[b][:, i * 128:(i + 1) * 128],
                                                                  xt[:, (g0 + i) * 128:(g0 + i + 1) * 128], self.ident[:]),
                              reads=[xB, self.identB], writes=[self.PSB[b]])
                    fw.op(fw.act, lambda: nc.scalar.mul(out=sg[:, g0:g0 + gn, :].rearrange("p a b -> p (a b)"),
                                                        in_=self.PS[b][:, 0:gn * 128], mul=cfg.alpha),
                          reads=[self.PSB[b]], writes=[sB])
                fw.dma(self.HA[:, :, t0:t0 + 128], sg[:], reads=[sB])
            fw.barrier()

    def phase_out(self):
        cfg, fw, nc = self.cfg, self.fw, self.nc
        DC = cfg.DC
        with ExitStack() as st:
            hin = Rot(self, st, "hin", [128, DC, 128], F32, 3)
            ost = Rot(self, st, "ost", [128, cfg.D], F32, 2)
            hs_ = {}

            def ldh(r_):
                hs_[r_] = hin.next()
                fw.dma(hs_[r_][0][:], self.HA[:, :, cfg.C + r_:cfg.C + r_ + 128], writes=[hs_[r_][1]])

            ldh(0)
            for r0 in range(0, cfg.S, 128):
                t0 = cfg.C + r0
                if r0 + 128 < cfg.S:
                    ldh(r0 + 128)
                ht, hB = hs_.pop(r0)
                ot, oB = ost.next()
                for g0 in range(0, DC, 4):
                    gn = min(4, DC - g0)
                    b = self.next_bank()
                    for i in range(gn):
                        fw.op(fw.pe, lambda: nc.tensor.transpose(self.PS[b][:, i * 128:(i + 1) * 128],
                                                                  ht[:, g0 + i, :], self.ident[:]),
                              reads=[hB, self.identB], writes=[self.PSB[b]])
                    fw.op(fw.act, lambda: nc.scalar.copy(out=ot[:, g0 * 128:(g0 + gn) * 128],
                                                         in_=self.PS[b][:, 0:gn * 128]),
                          reads=[self.PSB[b]], writes=[oB])
                fw.dma(self.out[r0:r0 + 128, :], ot[:], reads=[oB])

    def layer(self, l, lst):
        cfg, fw, nc = self.cfg, self.fw, self.nc
        DC = cfg.DC
        last = l == cfg.depth - 1
        j = l // 2
        is_rec = (l % 2 == 0)
        self.modv = self.sb(lst, "modv", [128, 6 * DC, 2])
        self.modvB = Buf()
        self.phase_mod(l)
        self.ln = self.sb(lst, "lncols", [128, 4, DC])
        self.lnB = Buf()
        a2 = 1.0 if last else cfg.alpha
        for i, (nm, sc) in enumerate((("ln_mix_g", cfg.alpha), ("ln_mix_b", cfg.alpha), ("ln_ffn_g", a2), ("ln_ffn_b", a2))):
            self.load_cols(self.ln[:, i, :], self.lnB, self.din[nm][l, :].rearrange("(n p) -> n p", p=128), DC, scale=sc)
        fw.barrier()
        if is_rec:
            self.rec_mixer(j, lst)
        else:
            self.att_mixer(j, lst, last)
        self.phase_ln(0)
        self.ffn(l)
        self.phase_ln(1)

    def mcol(self, grp, kc, ti):
        return self.modv[:, grp * self.cfg.DC + kc, (1 if ti == 0 else 0):(2 if ti == 0 else 1)]

    def phase_mod(self, l):
        cfg, fw, nc = self.cfg, self.fw, self.nc
        DC = cfg.DC
        N = 6 * cfg.D
        NG = 512
        wv = self.din["mod_w"][l].rearrange("(kc p) n -> p kc n", p=128)
        with ExitStack() as st:
            wsl = Rot(self, st, "modw", [128, DC, NG], F32, 2)
            mb = self.sb(st, "modb", [128, 6 * DC])
            mbB = Buf()
            self.load_cols(mb[:, :], mbB, self.din["mod_b"][l, :].rearrange("(n p) -> n p", p=128), 6 * DC)
            rowb = self.sb(st, "modrow", [2, N])
            rowB = Buf()
            nxt = wsl.next()
            fw.dma(nxt[0][:], wv[:, :, 0:NG], writes=[nxt[1]])
            for g in range(N // NG):
                wt, wB = nxt
                if g + 1 < N // NG:
                    nxt = wsl.next()
                    fw.dma(nxt[0][:], wv[:, :, (g + 1) * NG:(g + 2) * NG], writes=[nxt[1]])
                b = self.next_bank()
                for kc in range(DC):
                    fw.op(fw.pe, lambda: nc.tensor.matmul(self.PS[b][0:2, 0:NG], lhsT=self.silc[:, kc, :], rhs=wt[:, kc, :],
                                                          start=(kc == 0), stop=(kc == DC - 1)),
                          reads=[wB, self.silcB], writes=[self.PSB[b]])
                fw.op(fw.act, lambda: nc.scalar.copy(out=rowb[0:2, g * NG:(g + 1) * NG], in_=self.PS[b][0:2, 0:NG]),
                      reads=[self.PSB[b]], writes=[rowB])
            pb = self.next_bank()
            psm = self.PS[pb]
            for ch in range(6 * DC):
                fw.op(fw.pe, lambda: nc.tensor.transpose(psm[:, 2 * ch:2 * ch + 2], rowb[0:2, ch * 128:(ch + 1) * 128], self.ident[0:2, 0:2]),
                      reads=[rowB, self.identB], writes=[self.PSB[pb]])
            for jx in range(2):
                fw.op(fw.dve, lambda: nc.vector.tensor_tensor(
                    out=self.modv[:, :, jx], in0=psm[:, 0:12 * DC].rearrange("p (a b) -> p a b", b=2)[:, :, jx],
                    in1=mb[:, :], op=ALU.add), reads=[self.PSB[pb], mbB], writes=[self.modvB])
            for grp in (1, 4):
                fw.op(fw.dve, lambda: nc.vector.tensor_scalar(
                    out=self.modv[:, grp * DC:(grp + 1) * DC, :], in0=self.modv[:, grp * DC:(grp + 1) * DC, :],
                    scalar1=1.0, scalar2=1.0 / cfg.alpha, op0=ALU.add, op1=ALU.mult),
                    reads=[self.modvB], writes=[self.modvB])
            fw.barrier()

    def linear(self, KC, wlist, N, NG, sups, load_act, epi, modeA=None, sup_begin=None, pre=None, nst=2):
        cfg, fw, nc = self.cfg, self.fw, self.nc
        nw = len(wlist)
        wv = [w.rearrange("(kc p) n -> p kc n", p=128) for w in wlist]
        tiles = cfg.tiles
        with ExitStack() as st:
            Tmax = max(sum(tiles[i][1] for i in sup) for sup in sups)
            act = self.sb(st, "lin_act", [128, KC, Tmax], BF16)
            actB = [Buf() for _ in range(max(len(s) for s in sups))]
            wst = [self.sb(st, "lin_wst", [128, KC, nw, NG], F32) for _ in range(nst)]
            wstB = [[Buf() for _ in range(nw)] for _ in range(nst)]
            wbf = [self.sb(st, "lin_wbf", [128, KC, nw, NG], BF16) for _ in range(2)]
            wbfB = [[Buf(), Buf()], [Buf(), Buf()]]
            ng = N // NG

            def load_w(g):
                bf = g % 2
                sf = g % nst
                for wi in range(nw):
                    fw.dma(wst[sf][:, :, wi, :], wv[wi][:, :, g * NG:(g + 1) * NG], writes=[wstB[sf][wi]])
                k1 = max(1, KC // 4)
                fw.op(fw.pool, lambda: nc.gpsimd.tensor_copy(out=wbf[bf][:, 0:k1].rearrange("p a b c -> p (a b c)"),
                                                              in_=wst[sf][:, 0:k1].rearrange("p a b c -> p (a b c)")),
                      reads=wstB[sf], writes=[wbfB[bf][0]])
                fw.op(fw.act, lambda: nc.scalar.copy(out=wbf[bf][:, k1:KC].rearrange("p a b c -> p (a b c)"),
                                                     in_=wst[sf][:, k1:KC].rearrange("p a b c -> p (a b c)")),
                      reads=wstB[sf], writes=[wbfB[bf][1]])

            for sup in sups:
                load_w(0)
                slots, off = [], 0
                for si, ti in enumerate(sup):
                    n = tiles[ti][1]
                    slots.append((ti, off, n, actB[si]))
                    load_act(ti, act, off, n, actB[si])
                    off += n
                if sup_begin is not None:
                    sup_begin(sup)
                for g in range(ng):
                    if g + 1 < ng:
                        load_w(g + 1)
                    bf = g % 2
                    if modeA is not None and modeA(g):
                        for (ti, off, n, aB) in slots:
                            for blk in range(n // 128):
                                b = self.next_bank()
                                for kc in range(KC):
                                    fw.op(fw.pe, lambda: nc.tensor.matmul(
                                        self.PS[b][:, 0:NG], lhsT=act[:, kc, off + blk * 128:off + (blk + 1) * 128],
                                        rhs=wbf[bf][:, kc, 0, :], start=(kc == 0), stop=(kc == KC - 1)),
                                        reads=[wbfB[bf][0], wbfB[bf][1], aB], writes=[self.PSB[b]])
                                epi(("A", g, blk), ti, [self.PS[b][:, 0:NG]], [self.PSB[b]])
                        continue
                    units = [(nn, sl) for nn in range(NG // 128) for sl in slots]
                    pres = {}
                    for ui_, (nn, (ti, off, n, aB)) in enumerate(units):
                        nchunk = g * (NG // 128) + nn
                        if pre is not None:
                            for k_ in (ui_, ui_ + 1):
                                if k_ < len(units) and k_ not in pres:
                                    pres[k_] = pre(g * (NG // 128) + units[k_][0], units[k_][1][0])
                        if True:
                            banks = [self.next_bank() for _ in range(nw)]
                            for wi in range(nw):
                                b = banks[wi]
                                for kc in range(KC):
                                    fw.op(fw.pe, lambda: nc.tensor.matmul(
                                        self.PS[b][:, 0:n], lhsT=wbf[bf][:, kc, wi, nn * 128:(nn + 1) * 128],
                                        rhs=act[:, kc, off:off + n], start=(kc == 0), stop=(kc == KC - 1)),
                                        reads=[wbfB[bf][0], wbfB[bf][1], aB], writes=[self.PSB[b]])
                            if pre is not None:
                                epi(nchunk, ti, [self.PS[b][:, 0:n] for b in banks], [self.PSB[b] for b in banks], pres.pop(ui_))
                            else:
                                epi(nchunk, ti, [self.PS[b][:, 0:n] for b in banks], [self.PSB[b] for b in banks])
            fw.barrier()

    def mk_mod_loader(self, st, g_shift, g_scale, div=2):
        cfg, fw, nc = self.cfg, self.fw, self.nc
        DC = cfg.DC
        hc = max(1, (DC + div - 1) // div)
        stage = Rot(self, st, "modst", [128, hc, 512], F32, 2)

        def load(ti, act, off, n, aB):
            t0 = cfg.tiles[ti][0]
            for c0 in range(0, DC, hc):
                cn = min(hc, DC - c0)
                sg, sB = stage.next()
                fw.dma(sg[:, 0:cn, 0:n], self.HA[:, c0:c0 + cn, t0:t0 + n], writes=[sB])
                for k in range(cn):
                    kc = c0 + k
                    fw.op(fw.act, lambda: nc.scalar.activation(
                        out=act[:, kc, off:off + n], in_=sg[:, k, 0:n], func=AF.Identity,
                        scale=self.mcol(g_scale, kc, ti), bias=self.mcol(g_shift, kc, ti)),
                        reads=[sB, self.modvB], writes=[aB])
        return load

    def mk_direct_loader(self, src, KC):
        fw = self.fw

        def load(ti, act, off, n, aB):
            t0 = self.cfg.tiles[ti][0]
            fw.dma(act[:, :, off:off + n], src[:, 0:KC, t0:t0 + n], writes=[aB])
        return load

    def mk_resid_epi(self, st, g_gate):
        cfg, fw, nc = self.cfg, self.fw, self.nc
        hin = Rot(self, st, "rs_h", [128, 512], F32, 4)
        rout = Rot(self, st, "rs_o", [128, 512], F32, 4)

        def pre(nchunk, ti):
            t0, n = cfg.tiles[ti]
            ht, hB = hin.next()
            fw.dma(ht[:, 0:n], self.HA[:, nchunk, t0:t0 + n], writes=[hB])
            return ht, hB

        def epi(nchunk, ti, ps, psB, pr_):
            t0, n = cfg.tiles[ti]
            ht, hB = pr_
            ot, oB = rout.next()
            fw.op(fw.dve, lambda: nc.vector.scalar_tensor_tensor(
                out=ot[:, 0:n], in0=ps[0], scalar=self.mcol(g_gate, nchunk, ti), in1=ht[:, 0:n],
                op0=ALU.mult, op1=ALU.add), reads=[psB[0], hB, self.modvB], writes=[oB])
            fw.dma(self.R[:, nchunk, t0:t0 + n], ot[:, 0:n], reads=[oB])
        return epi, pre

    def phase_ln(self, which):
        cfg, fw, nc = self.cfg, self.fw, self.nc
        DC, D = cfg.DC, cfg.D
        gcol, bcol = 2 * which, 2 * which + 1
        R32 = mybir.dt.float32r if LN_FP32R else F32
        ntl = len(cfg.tiles)
        with ExitStack() as st:
            rin = [(self.sb(st, "ln_r", [128, DC, 512]), [Buf() for _ in range(DC)]) for _ in range(2)]
            rout = [(self.sb(st, "ln_o", [128, DC, 512]), [Buf() for _ in range(DC)]) for _ in range(2)]
            sqt = self.sb(st, "ln_sq", [128, DC, 512])
            sqB = Buf()
            small = Rot(self, st, "ln_s", [128, 512], F32, 8)
            banks = {}

            def ld(it):
                t0_, n_ = cfg.tiles[it]
                fw.dma(rin[it % 2][0][:, :, 0:n_], self.R[:, :, t0_:t0_ + n_], writes=rin[it % 2][1])

            def front(it):
                t0, n = cfg.tiles[it]
                rt, rBs = rin[it % 2]
                fw.op(fw.act, lambda: nc.scalar.activation(out=sqt[:, :, 0:n], in_=rt[:, :, 0:n], func=AF.Square),
                      reads=rBs, writes=[sqB])
                b1, b2 = self.next_bank(), self.next_bank()
                banks[it] = (b1, b2)
                for c in range(DC):
                    fw.op(fw.pe, lambda: nc.tensor.matmul(self.PS[b1][:, 0:n], lhsT=self.ones_f[:].bitcast(R32), rhs=rt[:, c, 0:n].bitcast(R32),
                                                          start=(c == 0), stop=(c == DC - 1)),
                          reads=[rBs[c], self.onesB], writes=[self.PSB[b1]])
                for c in range(DC):
                    fw.op(fw.pe, lambda: nc.tensor.matmul(self.PS[b2][:, 0:n], lhsT=self.ones_f[:].bitcast(R32), rhs=sqt[:, c, 0:n].bitcast(R32),
                                                          start=(c == 0), stop=(c == DC - 1)),
                          reads=[sqB, self.onesB], writes=[self.PSB[b2]])

            def back(it):
                t0, n = cfg.tiles[it]
                rt, rBs = rin[it % 2]
                ot, oBs = rout[it % 2]
                b1, b2 = banks.pop(it)
                mean, mB = small.next()
                fw.op(fw.act, lambda: nc.scalar.mul(out=mean[:, 0:n], in_=self.PS[b1][:, 0:n], mul=1.0 / D),
                      reads=[self.PSB[b1]], writes=[mB])
                msq, qB = small.next()
                fw.op(fw.pool, lambda: nc.gpsimd.tensor_tensor(out=msq[:, 0:n], in0=mean[:, 0:n], in1=mean[:, 0:n], op=ALU.mult),
                      reads=[mB], writes=[qB])
                var, vB = small.next()
                fw.op(fw.dve, lambda: nc.vector.scalar_tensor_tensor(out=var[:, 0:n], in0=self.PS[b2][:, 0:n], scalar=1.0 / D,
                                                                      in1=msq[:, 0:n], op0=ALU.mult, op1=ALU.subtract),
                      reads=[self.PSB[b2], qB], writes=[vB])
                fw.op(fw.act, lambda: nc.scalar.activation(out=var[:, 0:n], in_=var[:, 0:n], func=AF.Sqrt, bias=LN_EPS, scale=1.0),
                      reads=[vB], writes=[vB])
                rstd, sB = small.next()
                fw.op(fw.dve, lambda: nc.vector.reciprocal(out=rstd[:, 0:n], in_=var[:, 0:n]), reads=[vB], writes=[sB])
                for c in range(DC):
                    eng = fw.dve if c % 2 == 0 else fw.pool
                    e = nc.vector if c % 2 == 0 else nc.gpsimd
                    fw.op(eng, lambda: e.tensor_tensor(out=rt[:, c, 0:n], in0=rt[:, c, 0:n], in1=mean[:, 0:n], op=ALU.subtract),
                          reads=[mB], writes=[rBs[c]])
                    fw.op(eng, lambda: e.tensor_tensor(out=rt[:, c, 0:n], in0=rt[:, c, 0:n], in1=rstd[:, 0:n], op=ALU.mult),
                          reads=[sB], writes=[rBs[c]])
                    fw.op(fw.act, lambda: nc.scalar.activation(out=ot[:, c, 0:n], in_=rt[:, c, 0:n], func=AF.Identity,
                                                               scale=self.ln[:, gcol, c:c + 1], bias=self.ln[:, bcol, c:c + 1]),
                          reads=[rBs[c], self.lnB], writes=[oBs[c]])
                fw.dma(self.HA[:, :, t0:t0 + n], ot[:, :, 0:n], reads=oBs)

            ld(0)
            front(0)
            for it in range(ntl):
                if it + 1 < ntl:
                    ld(it + 1)
                    front(it + 1)
                back(it)
            fw.barrier()

    def ffn(self, l):
        cfg, fw, nc = self.cfg, self.fw, self.nc
        with ExitStack() as st:
            loader = self.mk_mod_loader(st, 3, 4)
            sg = Rot(self, st, "ff_sg", [128, 512], F32, 3)
            ho = Rot(self, st, "ff_ho", [128, 512], BF16, 4)

            def epi(nchunk, ti, ps, psB):
                t0, n = cfg.tiles[ti]
                s, sB = sg.next()
                fw.op(fw.act, lambda: nc.scalar.activation(out=s[:, 0:n], in_=ps[0], func=AF.Silu), reads=[psB[0]], writes=[sB])
                h, hB = ho.next()
                fw.op(fw.dve, lambda: nc.vector.tensor_tensor(out=h[:, 0:n], in0=ps[1], in1=s[:, 0:n], op=ALU.mult),
                      reads=[psB[1], sB], writes=[hB])
                fw.dma(self.AFF[:, nchunk, t0:t0 + n], h[:, 0:n], reads=[hB])

            self.linear(cfg.DC, [self.din["ffn_w_gate"][l], self.din["ffn_w_up"][l]], cfg.F, 256 if cfg.F % 256 == 0 else 128, cfg.sups(2304), loader, epi, nst=1)
        with ExitStack() as st:
            epi, pre = self.mk_resid_epi(st, 5)
            self.linear(cfg.FC, [self.din["ffn_w_down"][l]], cfg.D, 128, cfg.sups(1536),
                        self.mk_direct_loader(self.AFF, cfg.FC), epi, pre=pre, nst=1)

    def rec_mixer(self, j, lst):
        cfg, fw, nc = self.cfg, self.fw, self.nc
        DC, RC, T, C, BW = cfg.DC, cfg.RC, cfg.T, cfg.C, cfg.BW
        with ExitStack() as st:
            loader = self.mk_mod_loader(st, 0, 1)
            oy = Rot(self, st, "r1_y", [128, 512], BF16, 4)
            oz = Rot(self, st, "r1_z", [128, 512], F32, 4)

            def epi(nchunk, ti, ps, psB):
                t0, n = cfg.tiles[ti]
                if nchunk < RC:
                    o, oB = oy.next()
                    fw.op(fw.act, lambda: nc.scalar.activation(out=o[:, 0:n], in_=ps[0], func=AF.Gelu), reads=[psB[0]], writes=[oB])
                    fw.dma(self.Y[:, nchunk, t0:t0 + n], o[:, 0:n], reads=[oB])
                else:
                    o, oB = oz.next()
                    fw.op(fw.dve, lambda: nc.vector.tensor_copy(out=o[:, 0:n], in_=ps[0]), reads=[psB[0]], writes=[oB])
                    fw.dma(self.ZP[:, nchunk - RC, t0:t0 + n], o[:, 0:n], reads=[oB])

            self.linear(DC, [self.din["rec_w_in"][j]], 2 * cfg.Dr, 256, cfg.sups(2304), loader, epi)
        with ExitStack() as st:
            cw = self.sb(st, "cw", [128, 4, RC]); cwB = Buf()
            for tp in range(4):
                self.load_cols(cw[:, tp, :], cwB, self.din["rec_conv_w"][j, tp, :].rearrange("(n p) -> n p", p=128), RC)
            cb = self.sb(st, "cb", [128, RC]); cbB = Buf()
            self.load_cols(cb[:, :], cbB, self.din["rec_conv_b"][j, :].rearrange("(n p) -> n p", p=128), RC)
            gb = self.sb(st, "gb", [128, 2, 2, RC]); gbB = Buf()
            cl = self.sb(st, "cl", [128, 2, RC]); clB = Buf()
            for d in range(2):
                self.load_cols(gb[:, d, 0, :], gbB, self.din["rec_gate_a_b"][j, d, :].rearrange("(n p) -> n p", p=128), RC)
                self.load_cols(gb[:, d, 1, :], gbB, self.din["rec_gate_x_b"][j, d, :].rearrange("(n p) -> n p", p=128), RC)
                self.load_cols(cl[:, d, :], clB, self.din["rec_lambda"][j, d, :].rearrange("(n p) -> n p", p=128), RC)
            fw.op(fw.act, lambda: nc.scalar.activation(out=cl[:], in_=cl[:], func=AF.Exp, scale=-1.0), reads=[clB], writes=[clB])
            fw.op(fw.act, lambda: nc.scalar.activation(out=cl[:], in_=cl[:], func=AF.Ln, bias=1.0, scale=1.0), reads=[clB], writes=[clB])
            fw.op(fw.act, lambda: nc.scalar.mul(out=cl[:], in_=cl[:], mul=-RG_C), reads=[clB], writes=[clB])
            fw.barrier()
            with ExitStack() as s2:
                zpr = Rot(self, s2, "zp", [128, T], F32, 4)
                zr = Rot(self, s2, "z", [128, T], F32, 2)
                zbr = Rot(self, s2, "zb", [128, T], BF16, 2)
                ppr = Rot(self, s2, "zpp", [128, T], F32, 2)
                zps = {}

                def ldz(c_):
                    zps[c_] = zpr.next()
                    fw.dma(zps[c_][0][:], self.ZP[:, c_, :], writes=[zps[c_][1]])

                ldz(0)
                if RC > 1:
                    ldz(1)
                for c in range(RC):
                    if c + 2 < RC:
                        ldz(c + 2)
                    zp, pB = zps.pop(c)
                    z, zB = zr.next()
                    for (s0, s1) in ((0, C), (C, T)):
                        fw.op(fw.act, lambda: nc.scalar.activation(out=z[:, s0:s1], in_=zp[:, s0:s1], func=AF.Identity,
                                                                   scale=cw[:, 2, c:c + 1], bias=cb[:, c:c + 1]),
                              reads=[pB, cwB, cbB], writes=[zB])
                        for (tp, do, so, ln) in ((0, 2, 0, s1 - s0 - 2), (1, 1, 0, s1 - s0 - 1), (3, 0, 1, s1 - s0 - 1)):
                            if ln < 1024:
                                fw.op(fw.dve, lambda: nc.vector.scalar_tensor_tensor(
                                    out=z[:, s0 + do:s0 + do + ln], in0=zp[:, s0 + so:s0 + so + ln], scalar=cw[:, tp, c:c + 1],
                                    in1=z[:, s0 + do:s0 + do + ln], op0=ALU.mult, op1=ALU.add),
                                    reads=[pB, cwB], writes=[zB])
                            else:
                                pp, ppB = ppr.next()
                                fw.op(fw.act, lambda: nc.scalar.activation(out=pp[:, 0:ln], in_=zp[:, s0 + so:s0 + so + ln], func=AF.Copy,
                                                                           scale=cw[:, tp, c:c + 1]),
                                      reads=[pB, cwB], writes=[ppB])
                                fw.op(fw.dve, lambda: nc.vector.tensor_tensor(out=z[:, s0 + do:s0 + do + ln], in0=z[:, s0 + do:s0 + do + ln],
                                                                              in1=pp[:, 0:ln], op=ALU.add),
                                      reads=[ppB], writes=[zB])
                    zb, bB = zbr.next()
                    fw.op(fw.pool, lambda: nc.gpsimd.tensor_copy(out=zb[:], in_=z[:]), reads=[zB], writes=[bB])
                    fw.dma(self.Z[:, c, :], z[:], reads=[zB])
                    fw.dma(self.ZB[:, c, :], zb[:], reads=[bB])
                fw.barrier()
            with ExitStack() as s2:
                zt = self.sb(s2, "g_z", [128, T]); zB = Buf()
                A = [self.sb(s2, "g_a", [128, T]) for _ in range(2)]; AB = [Buf(), Buf()]
                Bt = [self.sb(s2, "g_b", [128, T]) for _ in range(2)]; BB = [Buf(), Buf()]
                tmp = self.sb(s2, "g_t", [128, T]); tB = Buf()
                NKMAX = 4
                zbk = [self.sb(s2, "g_zb", [128, T], BF16) for _ in range(NKMAX)]; zbkB = [Buf() for _ in range(NKMAX)]
                ytr = Rot(self, s2, "g_y", [128, T], BF16, 2)
                mt = self.sb(s2, "g_m", [128, T], BF16); mB = Buf()
                wgs = Rot(self, s2, "g_ws", [128, 4 * NKMAX, 128], F32, 2)
                wgb = Rot(self, s2, "g_wb", [128, 4 * NKMAX, 128], BF16, 2)
                gws = (self.din["rec_gate_a_w"], self.din["rec_gate_x_w"])

                def prefetch(c):
                    h0, h1 = (128 * c) // BW, (128 * c + 127) // BW
                    k0, k1 = (h0 * BW) // 128, ((h1 + 1) * BW - 1) // 128
                    ks = list(range(k0, k1 + 1))
                    assert len(ks) <= NKMAX
                    ws, wsB = wgs.next()
                    wb, wbB = wgb.next()
                    fw.op(fw.pool, lambda: nc.gpsimd.memset(ws[:], 0.0), writes=[wsB])
                    for d in range(2):
                        for gt in range(2):
                            for ki, k in enumerate(ks):
                                tix = (d * 2 + gt) * NKMAX + ki
                                for h in range(h0, h1 + 1):
                                    rl, rh = max(h * BW, 128 * k), min((h + 1) * BW, 128 * k + 128)
                                    c_l, c_h = max(h * BW, 128 * c), min((h + 1) * BW, 128 * c + 128)
                                    if rl >= rh or c_l >= c_h:
                                        continue
                                    fw.dma(ws[rl - 128 * k:rh - 128 * k, tix, c_l - 128 * c:c_h - 128 * c],
                                           gws[gt][j, d, h, rl - h * BW:rh - h * BW, c_l - h * BW:c_h - h * BW],
                                           writes=[Buf()], reads=[wsB])
                    fw.op(fw.pool, lambda: nc.gpsimd.tensor_copy(out=wb[:], in_=ws[:]), reads=[], writes=[wsB, wbB])
                    for ki, k in enumerate(ks):
                        fw.dma(zbk[ki][:], self.ZB[:, k, :], writes=[zbkB[ki]])
                    yt, yB = ytr.next()
                    fw.dma(yt[:], self.Y[:, c, :], writes=[yB])
                    return ks, wb, wbB, yt, yB

                nxt = prefetch(0)
                for c in range(RC):
                    ks, wb, wbB, yt, yB = nxt
                    fw.dma(zt[:], self.Z[:, c, :], writes=[zB])
                    for tix_, (t0, n) in enumerate(cfg.tiles):
                        for d in range(2):
                            banks = []
                            for gt in range(2):
                                b = self.next_bank()
                                banks.append(b)
                                for ki, k in enumerate(ks):
                                    tix = (d * 2 + gt) * NKMAX + ki
                                    fw.op(fw.pe, lambda: nc.tensor.matmul(self.PS[b][:, 0:n], lhsT=wb[:, tix, :], rhs=zbk[ki][:, t0:t0 + n],
                                                                          start=(ki == 0), stop=(ki == len(ks) - 1)),
                                          reads=[wbB, zbkB[ki]], writes=[self.PSB[b]])
                            fw.op(fw.act, lambda: nc.scalar.activation(out=A[d][:, t0:t0 + n], in_=self.PS[banks[0]][:, 0:n], func=AF.Sigmoid,
                                                                       bias=gb[:, d, 0, c:c + 1], scale=1.0),
                                  reads=[self.PSB[banks[0]], gbB], writes=[AB[d]])
                            fw.op(fw.act, lambda: nc.scalar.activation(out=Bt[d][:, t0:t0 + n], in_=self.PS[banks[1]][:, 0:n], func=AF.Sigmoid,
                                                                       bias=gb[:, d, 1, c:c + 1], scale=1.0),
                                  reads=[self.PSB[banks[1]], gbB], writes=[BB[d]])
                    if c + 1 < RC:
                        nxt = prefetch(c + 1)
                    fw.op(fw.dve, lambda: nc.vector.tensor_tensor(out=Bt[0][:], in0=Bt[0][:], in1=zt[:], op=ALU.mult),
                          reads=[zB], writes=[BB[0]])
                    fw.op(fw.pool, lambda: nc.gpsimd.tensor_tensor(out=Bt[1][:], in0=Bt[1][:], in1=zt[:], op=ALU.mult),
                          reads=[zB], writes=[BB[1]])
                    for d in range(2):
                        fw.op(fw.act, lambda: nc.scalar.activation(out=A[d][:], in_=A[d][:], func=AF.Exp, scale=cl[:, d, c:c + 1]),
                              reads=[clB], writes=[AB[d]])
                    for d, (sc_, sB_) in enumerate(((tmp, tB), (zt, zB))):
                        fw.op(fw.act, lambda: nc.scalar.activation(out=sc_[:], in_=A[d][:], func=AF.Square), reads=[AB[d]], writes=[sB_])
                        fw.op(fw.act, lambda: nc.scalar.activation(out=sc_[:], in_=sc_[:], func=AF.Sqrt, scale=-1.0, bias=1.0),
                              reads=[sB_], writes=[sB_])
                    fw.op(fw.dve, lambda: nc.vector.tensor_tensor(out=Bt[0][:], in0=Bt[0][:], in1=tmp[:], op=ALU.mult),
                          reads=[tB], writes=[BB[0]])
                    fw.op(fw.dve, lambda: nc.vector.tensor_tensor_scan(out=tmp[:, :], data0=A[0][:, :], data1=Bt[0][:, :], initial=0.0,
                                                                       op0=ALU.mult, op1=ALU.add),
                          reads=[AB[0], BB[0]], writes=[tB])
                    fw.op(fw.dve, lambda: nc.vector.tensor_tensor(out=Bt[1][:], in0=Bt[1][:], in1=zt[:], op=ALU.mult),
                          reads=[zB], writes=[BB[1]])
                    fw.op(fw.dve, lambda: nc.vector.tensor_tensor_scan(out=zt[:, 0:C][:, ::-1], data0=A[1][:, 0:C][:, ::-1],
                                                                       data1=Bt[1][:, 0:C][:, ::-1], initial=0.0,
                                                                       op0=ALU.mult, op1=ALU.add),
                          reads=[AB[1], BB[1]], writes=[zB])
                    fw.op(fw.dve, lambda: nc.vector.tensor_tensor_scan(out=zt[:, C:T][:, ::-1], data0=A[1][:, C:T][:, ::-1],
                                                                       data1=Bt[1][:, C:T][:, ::-1], initial=zt[:, 0:1],
                                                                       op0=ALU.mult, op1=ALU.add),
                          reads=[AB[1], BB[1]], writes=[zB])
                    fw.op(fw.dve, lambda: nc.vector.tensor_tensor(out=tmp[:], in0=tmp[:], in1=zt[:], op=ALU.add), reads=[zB], writes=[tB])
                    fw.op(fw.pool, lambda: nc.gpsimd.tensor_tensor(out=mt[:], in0=tmp[:], in1=yt[:], op=ALU.mult),
                          reads=[tB, yB], writes=[mB])
                    fw.dma(self.M[:, c, :], mt[:], reads=[mB])
                fw.barrier()
        with ExitStack() as st:
            epi, pre = self.mk_resid_epi(st, 2)
            self.linear(RC, [self.din["rec_w_out"][j]], cfg.D, 256, cfg.sups(2304), self.mk_direct_loader(self.M, RC), epi, pre=pre, nst=1)

    def att_mixer(self, j, lst, last):
        cfg, fw, nc = self.cfg, self.fw, self.nc
        DC, T, C, S, H, KV, G = cfg.DC, cfg.T, cfg.C, cfg.S, cfg.H, cfg.KV, cfg.G
        GW = G * 128
        with ExitStack() as ast:
            with ExitStack() as st:
                loader = self.mk_mod_loader(st, 0, 1, div=2)
                sups = cfg.sups(2304)
                vo = Rot(self, st, "vo", [128, 256], BF16, 4)
                Tmax = max(sum(cfg.tiles[i][1] for i in sup) for sup in sups)
                cs = self.sb(st, "cos", [128, Tmax]); sn = self.sb(st, "sin", [128, Tmax]); csB = Buf()
                state = {}

                def sup_begin(sup):
                    off = 0
                    state["off"] = {}
                    for ti in sup:
                        t0, n = cfg.tiles[ti]
                        state["off"][ti] = off
                        if ti != 0:
                            fw.dma(cs[:, off:off + n], self.din["cos2"][:, t0 - C:t0 - C + n], writes=[csB])
                            fw.dma(sn[:, off:off + n], self.din["sinx"][:, t0 - C:t0 - C + n], writes=[csB])
                        off += n

                t1r = Rot(self, st, "rp1", [128, 512], F32, 3)
                t2r = Rot(self, st, "rp2", [128, 512], F32, 3)
                qo = Rot(self, st, "qo", [128, 512], BF16, 4)
                NGQ = 256
                gv0 = (H + KV) * 128 // NGQ

                def epi(nchunk, ti, ps, psB):
                    t0, n = cfg.tiles[ti]
                    if isinstance(nchunk, tuple):
                        _, g, blk = nchunk
                        col0 = (g - gv0) * NGQ
                        vt, vB = vo.next()
                        fw.op(fw.act, lambda: nc.scalar.copy(out=vt[:, 0:NGQ], in_=ps[0]), reads=[psB[0]], writes=[vB])
                        fw.dma(self.VD[:, t0 // 128 + blk, col0:col0 + NGQ], vt[:, 0:NGQ], reads=[vB])
                        return
                    isq = nchunk < H
                    dst, dB = qo.next()
                    dst = dst[:, 0:n]
                    if ti == 0:
                        fw.op(fw.act, lambda: nc.scalar.copy(out=dst, in_=ps[0]), reads=[psB[0]], writes=[dB])
                    else:
                        o = state["off"][ti]
                        t1, t1B = t1r.next()
                        t2, t2B = t2r.next()
                        fw.op(fw.dve, lambda: nc.vector.tensor_tensor(out=t1[:, 0:n], in0=ps[0], in1=cs[:, o:o + n], op=ALU.mult),
                              reads=[psB[0], csB], writes=[t1B])
                        fw.op(fw.dve, lambda: nc.vector.tensor_tensor(out=t2[0:64, 0:n], in0=ps[0][64:128, :], in1=sn[64:128, o:o + n], op=ALU.mult),
                              reads=[psB[0], csB], writes=[t2B])
                        fw.op(fw.dve, lambda: nc.vector.tensor_tensor(out=t2[64:128, 0:n], in0=ps[0][0:64, :], in1=sn[0:64, o:o + n], op=ALU.mult),
                              reads=[psB[0], csB], writes=[t2B])
                        fw.op(fw.pool, lambda: nc.gpsimd.tensor_tensor(out=dst, in0=t1[:, 0:n], in1=t2[:, 0:n], op=ALU.add),
                              reads=[t1B, t2B], writes=[dB])
                    if isq:
                        fw.dma(self.Q[:, nchunk, t0:t0 + n], dst, reads=[dB])
                    else:
                        fw.dma(self.KD[:, nchunk - H, t0:t0 + n], dst, reads=[dB])

                self.linear(DC, [self.din["att_w_qkv"][j]], cfg.NQKV, NGQ, sups, loader, epi,
                            modeA=lambda g: g >= gv0, sup_begin=sup_begin)
            with ExitStack() as st:
                kT = self.sb(st, "kT", [128, KV, T], BF16)
                kTB = Buf()
                V = self.sb(st, "V", [128, T // 128, KV * 128], BF16)
                VB = Buf()
                fw.dma(kT[:], self.KD[:, :, :], writes=[kTB])
                fw.dma(V[:], self.VD[:, :, :], writes=[VB])
                es = self.sb(st, "esink", [128, H]); esB = Buf()
                fw.dma(es[:], self.din["att_sink"][j:j + 1, :].to_broadcast([128, H]), writes=[esB])
                fw.op(fw.act, lambda: nc.scalar.activation(out=es[:], in_=es[:], func=AF.Exp), reads=[esB], writes=[esB])
                esf = self.sb(st, "esf", [128, H, 128]); esfB = Buf()
                fw.op(fw.pool, lambda: nc.gpsimd.memset(esf[:], 0.0), writes=[esfB])
                for h in range(H):
                    fw.op(fw.pool, lambda: nc.gpsimd.tensor_scalar(out=esf[:, h, :], in0=esf[:, h, :], scalar1=es[:, h:h + 1], scalar2=None,
                                                                   op0=ALU.add), reads=[esB], writes=[esfB])
                mk32 = self.sb(st, "mk32", [128, 2, 128]); mk = self.sb(st, "mk", [128, 2, G, 128], BF16); mkB = Buf()
                fw.dma(mk32[:, 0, :], self.din["mask_prev"][:, :], writes=[mkB])
                fw.dma(mk32[:, 1, :], self.din["mask_next"][:, :], writes=[mkB])
                fw.op(fw.pool, lambda: nc.gpsimd.tensor_scalar(out=mk32[:], in0=mk32[:], scalar1=-1.0, scalar2=30000.0,
                                                               op0=ALU.add, op1=ALU.mult), reads=[mkB], writes=[mkB])
                for m in range(2):
                    for g in range(G):
                        fw.op(fw.pool, lambda: nc.gpsimd.tensor_copy(out=mk[:, m, g, :], in_=mk32[:, m, :]), reads=[mkB], writes=[mkB])
                idb = self.sb(st, "identb", [128, 128], BF16); idbB = Buf()
                fw.op(fw.pool, lambda: nc.gpsimd.tensor_copy(out=idb[:], in_=self.ident[:]), reads=[self.identB], writes=[idbB])
                qr = Rot(self, st, "aq", [128, H, 128], BF16, 4)
                pr = Rot(self, st, "ap", [128, GW], BF16, 5)
                orr = Rot(self, st, "ao", [128, H, 128], BF16, 2)
                dr = Rot(self, st, "ad", [128, GW], F32, 2)
                NB = S // 128
                blocks = []
                if True:
                    for qb in range(C // 128):
                        blocks.append((qb * 128, [(kb * 128, None) for kb in range(C // 128)]))
                for jb in range(NB):
                    ch = []
                    for o_, mt_ in ((-1, 0), (0, None), (1, 1)):
                        if 0 <= jb + o_ < NB:
                            ch.append((C + (jb + o_) * 128, mt_))
                    ch += [(kb * 128, None) for kb in range(C // 128)]
                    blocks.append((C + jb * 128, ch))
                units = []
                for bi, (qoff, ch) in enumerate(blocks):
                    for kh in range(KV):
                        for ci, (koff, mt_) in enumerate(ch):
                            units.append((bi, kh, ci, koff, mt_, len(ch)))
                qtiles = {}

                def ensure_q(bi):
                    if bi < len(blocks) and bi not in qtiles:
                        qt, qB = qr.next()
                        qoff = blocks[bi][0]
                        fw.dma(qt[:], self.Q[:, :, qoff:qoff + 128], writes=[qB])
                        qtiles[bi] = (qt, qB)

                sc = cfg.HD ** -0.5
                SBANK = [0, 1, 2, 3]
                pts = {}

                def emitS(ui):
                    bi, kh, ci, koff, mt_, nch = units[ui]
                    ensure_q(bi)
                    ensure_q(bi + 1)
                    qt, qB = qtiles[bi]
                    b = SBANK[ui % 4]
                    fw.op(fw.pe, lambda: nc.tensor.matmul(self.PS[b][:, 0:GW], lhsT=kT[:, kh, koff:koff + 128],
                                                          rhs=qt[:, kh * G:(kh + 1) * G, :].rearrange("p a b -> p (a b)"),
                                                          start=True, stop=(mt_ is None)),
                          reads=[kTB, qB], writes=[self.PSB[b]])
                    if mt_ is not None:
                        fw.op(fw.pe, lambda: nc.tensor.matmul(self.PS[b][:, 0:GW], lhsT=idb[:], rhs=mk[:, mt_, :, :].rearrange("p a b -> p (a b)"),
                                                              start=False, stop=True),
                              reads=[mkB, idbB], writes=[self.PSB[b]])

                cur = {}

                def emitPV(ui):
                    bi, kh, ci, koff, mt_, nch = units[ui]
                    b = SBANK[ui % 4]
                    pt, pB = pr.next()
                    fw.op(fw.act, lambda: nc.scalar.activation(out=pt[:], in_=self.PS[b][:, 0:GW], func=AF.Exp, scale=sc),
                          reads=[self.PSB[b]], writes=[pB])
                    pair = bi * KV + kh
                    ob, db = 4 + pair % 2, 6 + pair % 2
                    fw.op(fw.pe, lambda: nc.tensor.matmul(self.PS[ob][:, 0:GW], lhsT=V[:, koff // 128, kh * 128:(kh + 1) * 128], rhs=pt[:],
                                                          start=(ci == 0), stop=(ci == nch - 1)),
                          reads=[VB, pB], writes=[self.PSB[ob]])
                    fw.op(fw.pe, lambda: nc.tensor.matmul(self.PS[db][:, 0:GW], lhsT=self.ones_b[:], rhs=pt[:],
                                                          start=(ci == 0), stop=(ci == nch - 1)),
                          reads=[self.onesB, pB], writes=[self.PSB[db]])
                    if ci == nch - 1:
                        if kh == 0:
                            cur["o"] = orr.next()
                        ot, oB = cur["o"]
                        dt_, dB = dr.next()
                        fw.op(fw.dve, lambda: nc.vector.tensor_tensor(out=dt_[:], in0=self.PS[db][:, 0:GW],
                                                                      in1=esf[:, kh * G:(kh + 1) * G, :].rearrange("p a b -> p (a b)"), op=ALU.add),
                              reads=[self.PSB[db], esfB], writes=[dB])
                        fw.op(fw.dve, lambda: nc.vector.reciprocal(out=dt_[:], in_=dt_[:]), reads=[dB], writes=[dB])
                        fw.op(fw.dve, lambda: nc.vector.tensor_tensor(out=ot[:, kh * G:(kh + 1) * G, :].rearrange("p a b -> p (a b)"),
                                                                      in0=self.PS[ob][:, 0:GW], in1=dt_[:], op=ALU.mult),
                              reads=[self.PSB[ob], dB], writes=[oB])
                        if kh == KV - 1:
                            qoff = blocks[bi][0]
                            fw.dma(self.O[:, :, qoff:qoff + 128], ot[:], reads=[oB])
                            qtiles.pop(bi, None)

                emitS(0)
                if len(units) > 1:
                    emitS(1)
                for ui in range(len(units)):
                    if ui + 2 < len(units):
                        emitS(ui + 2)
                    emitPV(ui)
                fw.barrier()
        with ExitStack() as st:
            epi, pre = self.mk_resid_epi(st, 2)
            self.linear(DC, [self.din["att_w_o"][j]], cfg.D, 256, cfg.sups(2304), self.mk_direct_loader(self.O, DC), epi, pre=pre)


def input_shapes(cfg):
    D, S, C, Dr, F, L = cfg.D, cfg.S, cfg.C, cfg.Dr, cfg.F, cfg.depth
    return {
        "x": (S, D), "ctx": (C, D), "c": (1, D), "c_ctx": (1, D),
        "mod_w": (L, D, 6 * D), "mod_b": (L, 6 * D),
        "ln_mix_g": (L, D), "ln_mix_b": (L, D), "ln_ffn_g": (L, D), "ln_ffn_b": (L, D),
        "ffn_w_gate": (L, D, F), "ffn_w_up": (L, D, F), "ffn_w_down": (L, F, D),
        "rec_w_in": (cfg.n_rec, D, 2 * Dr), "rec_conv_w": (cfg.n_rec, 4, Dr), "rec_conv_b": (cfg.n_rec, Dr),
        "rec_gate_a_w": (cfg.n_rec, 2, cfg.NBLK, cfg.BW, cfg.BW), "rec_gate_a_b": (cfg.n_rec, 2, Dr),
        "rec_gate_x_w": (cfg.n_rec, 2, cfg.NBLK, cfg.BW, cfg.BW), "rec_gate_x_b": (cfg.n_rec, 2, Dr),
        "rec_lambda": (cfg.n_rec, 2, Dr), "rec_w_out": (cfg.n_rec, Dr, D),
        "att_w_qkv": (max(cfg.n_att, 1), D, cfg.NQKV), "att_sink": (max(cfg.n_att, 1), cfg.H),
        "att_w_o": (max(cfg.n_att, 1), D, D),
        "ident": (128, 128), "cos2": (128, S), "sinx": (128, S), "mask_prev": (128, 128), "mask_next": (128, 128),
    }


def constants(cfg):
    S = cfg.S
    t = np.arange(S)
    row = (t // cfg.grid_w).astype(np.float32)
    col = (t % cfg.grid_w).astype(np.float32)
    axis_dim = cfg.HD // 2
    inv = (cfg.rope_base ** (-np.arange(0, axis_dim, 2, dtype=np.float32) / axis_dim)).astype(np.float32)
    ang = np.concatenate([row[:, None] * inv, col[:, None] * inv], axis=-1).astype(np.float32)
    cos, sin = np.cos(ang).T.astype(np.float32), np.sin(ang).T.astype(np.float32)
    p = np.arange(128)[:, None]
    i = np.arange(128)[None, :]
    return {
        "ident": np.eye(128, dtype=np.float32),
        "cos2": np.ascontiguousarray(np.concatenate([cos, cos], 0)),
        "sinx": np.ascontiguousarray(np.concatenate([sin, -sin], 0)),
        "mask_prev": (p >= i).astype(np.float32),
        "mask_next": (p <= i).astype(np.float32),
    }


_CACHE = {}


def run(cfg, inputs, n_cores):
    key = (cfg.D, cfg.S, cfg.depth)
    if key not in _CACHE:
        _CACHE[key] = KB(cfg).build()
    nc = _CACHE[key]
    consts = constants(cfg)
    f = lambda a: np.ascontiguousarray(np.asarray(a, dtype=np.float32))
    shared = {k: f(v) for k, v in inputs.items() if k not in ("x", "c", "ctx", "c_ctx")}
    shared.update(consts)
    shared["c_ctx"] = f(inputs["c_ctx"]).reshape(1, -1)
    in_maps = []
    for b in range(n_cores):
        m = dict(shared)
        m["x"] = f(inputs["x"][b])
        m["ctx"] = f(inputs["ctx"][b])
        m["c"] = f(inputs["c"][b:b + 1])
        in_maps.append(m)
    res = run_bass_kernel_spmd(nc, in_maps, core_ids=list(range(n_cores)))
    return np.stack([np.asarray(r["out"], dtype=np.float32) for r in res.results], 0)


def kernel(**inputs):
    cfg = Cfg()
    B = inputs["x"].shape[0]
    return run(cfg, inputs, B)
```
